# Optimizing a Trainium2 kernel written in Bass

```python
import math
import jax
import jax.numpy as jnp
from jax import lax
import numpy as np

D_MODEL = 1024
BATCH = 2
SEQ = 16384
DEPTH = 4
DEC_BATCH = 16
DEC_SEQ = 2048
PAST_LEN = 128

GRID_W = 64
D_FF = 4 * D_MODEL
EPS = 1e-6
NEG_INF = -1e30
Q_BLOCK = 128

A_HEADS = 8
A_KV_HEADS = 2
A_HEAD_DIM = 64
ROPE_THETA = 10000.0
B_WIDTH = 512
B_BLOCKS = 8
B_BLOCK_DIM = B_WIDTH // B_BLOCKS
B_CONV = 4
RG_C = 8.0
C_HEADS = 4
C_HEAD_DIM = 128
C_WIDTH = C_HEADS * C_HEAD_DIM
C_CONV = 4
C_CHUNK = 64
D_GROUPS = ((128, 1), (512, 4), (2048, 16))
D_HEADS_PER_GROUP = 4
D_HEAD_DIM = 64
D_NHEADS = len(D_GROUPS) * D_HEADS_PER_GROUP
D_WIDTH = D_NHEADS * D_HEAD_DIM
N_BUCKETS = 32
MAX_DISTANCE = 1024

A_Q = A_HEADS * A_HEAD_DIM
A_KV = A_KV_HEADS * A_HEAD_DIM
EVEN_IN = A_Q + 2 * A_KV + 2 * B_WIDTH
EVEN_OUT = A_Q + B_WIDTH
ODD_IN = 4 * C_WIDTH + 4 * C_HEADS + 3 * D_WIDTH
ODD_OUT = C_WIDTH + D_HEADS_PER_GROUP * D_HEAD_DIM
N_EVEN = (DEPTH + 1) // 2
N_ODD = DEPTH // 2

kernel_name = 'hybrid_bidir_encoder'


def _split(x, sizes):
    return jnp.split(x, np.cumsum(sizes)[:-1].tolist(), axis=-1)


def rms_norm(x, g):
    xf = x.astype(jnp.float32)
    y = xf * lax.rsqrt(jnp.mean(xf * xf, axis=-1, keepdims=True) + EPS)
    return (y * g.astype(jnp.float32)).astype(x.dtype)


def l2_norm(x):
    return x * lax.rsqrt(jnp.sum(x * x, axis=-1, keepdims=True) + EPS)


def centred_dwconv(x, w, b=None):
    W = w.shape[0]
    left = W // 2
    S = x.shape[1]
    xp = jnp.pad(x, ((0, 0), (left, W - 1 - left), (0, 0)))
    y = xp[:, 0:S] * w[0]
    for j in range(1, W):
        y = y + xp[:, j:j + S] * w[j]
    if b is not None:
        y = y + b
    return y


def axial_rope_tables(S):
    rows = S // GRID_W
    row = jnp.repeat(jnp.arange(rows, dtype=jnp.float32), GRID_W)
    col = jnp.tile(jnp.arange(GRID_W, dtype=jnp.float32), rows)
    n_freq = A_HEAD_DIM // 4
    inv = ROPE_THETA ** (-jnp.arange(n_freq, dtype=jnp.float32) / n_freq)
    ang = jnp.stack([row[:, None] * inv, col[:, None] * inv], axis=1)
    return jnp.cos(ang), jnp.sin(ang)


def apply_axial_rope(x, cos, sin):
    Bn, S, H, dh = x.shape
    xr = x.astype(jnp.float32).reshape(Bn, S, H, 2, 2, dh // 4)
    a, b = xr[..., 0, :], xr[..., 1, :]
    c = cos[None, :, None]
    s = sin[None, :, None]
    out = jnp.stack([a * c - b * s, a * s + b * c], axis=-2)
    return out.reshape(Bn, S, H, dh)


def gqa_block_attention(q, k, v):
    Bn, S, _, dh = q.shape
    G = A_HEADS // A_KV_HEADS
    nb = S // Q_BLOCK
    qb = q.reshape(Bn, nb, Q_BLOCK, A_KV_HEADS, G, dh).transpose(1, 0, 2, 3, 4, 5)
    scale = dh ** -0.5

    def block(qi):
        s = jnp.einsum('bqhgd,bkhd->bhgqk', qi, k).astype(jnp.float32) * scale
        p = jax.nn.softmax(s, axis=-1)
        return jnp.einsum('bhgqk,bkhd->bqhgd', p.astype(v.dtype), v)

    o = lax.map(block, qb)
    return o.transpose(1, 0, 2, 3, 4, 5).reshape(Bn, S, A_HEADS * dh)


def _lin_combine(left, right):
    a1, b1 = left
    a2, b2 = right
    return a1 * a2, a2 * b1 + b2


def rglru_scan(x, wr, br, wi, bi, lam):
    Bn, S, W = x.shape
    xb = x.reshape(Bn, S, B_BLOCKS, B_BLOCK_DIM)
    r = jax.nn.sigmoid(jnp.einsum('bsnd,nde->bsne', xb, wr.astype(jnp.float32)).reshape(Bn, S, W) + br.astype(jnp.float32))
    i = jax.nn.sigmoid(jnp.einsum('bsnd,nde->bsne', xb, wi.astype(jnp.float32)).reshape(Bn, S, W) + bi.astype(jnp.float32))
    log_a = -RG_C * r * jax.nn.softplus(-lam.astype(jnp.float32))
    a = jnp.exp(log_a)
    u = jnp.sqrt(-jnp.expm1(2.0 * log_a)) * (i * x)
    _, h = lax.associative_scan(_lin_combine, (a, u), axis=1)
    return h


def _even_mixer(h, w_in, w_out, q_gain, k_gain, conv_w, conv_b, wr, br, wi, bi, lam):
    Bn, S, _ = h.shape
    dt = h.dtype
    proj = h @ w_in
    q, k, v, xr, gr = _split(proj, (A_Q, A_KV, A_KV, B_WIDTH, B_WIDTH))
    cos, sin = axial_rope_tables(S)
    q = rms_norm(q.reshape(Bn, S, A_HEADS, A_HEAD_DIM), q_gain)
    k = rms_norm(k.reshape(Bn, S, A_KV_HEADS, A_HEAD_DIM), k_gain)
    q = apply_axial_rope(q, cos, sin).astype(dt)
    k = apply_axial_rope(k, cos, sin).astype(dt)
    v = v.reshape(Bn, S, A_KV_HEADS, A_HEAD_DIM)
    y_a = gqa_block_attention(q, k, v)
    xc = centred_dwconv(xr.astype(jnp.float32), conv_w.astype(jnp.float32), conv_b.astype(jnp.float32))
    y_f = rglru_scan(xc, wr[0], br[0], wi[0], bi[0], lam[0])
    y_r = jnp.flip(rglru_scan(jnp.flip(xc, 1), wr[1], br[1], wi[1], bi[1], lam[1]), 1)
    y_b = (y_f + y_r) * jax.nn.gelu(gr.astype(jnp.float32))
    return jnp.concatenate([y_a, y_b.astype(dt)], axis=-1) @ w_out


def gated_delta_chunked(q, k, v, g, beta):
    Bn, S, H, dk = q.shape
    dv = v.shape[-1]
    C = C_CHUNK
    n = S // C

    def chunks(t):
        t = t.reshape(Bn, n, C, H, *t.shape[3:])
        return jnp.moveaxis(t, (1, 3), (0, 2))

    qc = chunks(q * dk ** -0.5)
    kc = chunks(k)
    vc = chunks(v)
    gc = jnp.cumsum(chunks(g), axis=-1)
    bc = chunks(beta)
    tril = jnp.tril(jnp.ones((C, C), dtype=bool))
    strict = jnp.tril(jnp.ones((C, C), dtype=bool), -1)
    decay = jnp.exp(jnp.where(tril, gc[..., :, None] - gc[..., None, :], -jnp.inf))
    kb = kc * bc[..., None]
    L = jnp.where(strict, jnp.einsum('nbhid,nbhjd->nbhij', kb, kc) * decay, 0.0)
    eye = jnp.eye(C, dtype=L.dtype)
    rhs = jnp.concatenate([vc * bc[..., None], kb * jnp.exp(gc)[..., None]], axis=-1)
    sol = lax.linalg.triangular_solve(L + eye, rhs, left_side=True, lower=True, unit_diagonal=True)
    u = sol[..., :dv]
    w = sol[..., dv:]
    intra = jnp.where(tril, jnp.einsum('nbhid,nbhjd->nbhij', qc, kc) * decay, 0.0)

    def step(state, xs):
        q_i, k_i, u_i, w_i, g_i, a_i = xs
        v_new = u_i - jnp.einsum('bhcd,bhde->bhce', w_i, state)
        o = jnp.einsum('bhcd,bhde->bhce', q_i * jnp.exp(g_i)[..., None], state) + jnp.einsum('bhij,bhje->bhie', a_i, v_new)
        g_last = g_i[..., -1:]
        state = state * jnp.exp(g_last)[..., None] + jnp.einsum('bhcd,bhce->bhde', k_i * jnp.exp(g_last - g_i)[..., None], v_new)
        return state, o

    state0 = jnp.zeros((Bn, H, dk, dv), q.dtype)
    _, o = lax.scan(step, state0, (qc, kc, u, w, gc, intra))
    return jnp.moveaxis(o, (0, 2), (1, 3)).reshape(Bn, S, H, dv)


def t5_bucket(rel):
    nb = N_BUCKETS // 2
    max_exact = nb // 2
    n = np.abs(rel)
    large = max_exact + (np.log(np.maximum(n, 1) / max_exact) / math.log(MAX_DISTANCE / max_exact) * (nb - max_exact)).astype(np.int64)
    large = np.minimum(large, nb - 1)
    return (np.where(rel > 0, nb, 0) + np.where(n < max_exact, n, large)).astype(np.int32)


def banded_dilated_attention(q, k, v, bias, dil, steps):
    Bn, S, H, dh = q.shape
    M = S // dil
    blk = steps
    nb = -(-M // blk)
    Mp = nb * blk
    Z = Bn * dil

    def to_sub(t):
        return t.reshape(Bn, M, dil, H, dh).transpose(0, 2, 1, 3, 4).reshape(Z, M, H, dh)

    def key_windows(t):
        tp = jnp.pad(to_sub(t), ((0, 0), (blk, Mp - M + blk), (0, 0), (0, 0))).reshape(Z, nb + 2, blk, H, dh)
        return jnp.concatenate([tp[:, :-2], tp[:, 1:-1], tp[:, 2:]], axis=2)

    qs = jnp.pad(to_sub(q), ((0, 0), (0, Mp - M), (0, 0), (0, 0))).reshape(Z, nb, blk, H, dh)
    kw = key_windows(k)
    vw = key_windows(v)
    delta = jnp.arange(3 * blk)[None, :] - blk - jnp.arange(blk)[:, None]
    in_band = jnp.abs(delta) <= steps
    kpos = jnp.arange(nb)[:, None] * blk + jnp.arange(3 * blk)[None, :] - blk
    valid = (kpos >= 0) & (kpos < M)
    mask = in_band[None] & valid[:, None, :]
    bias_m = bias.astype(jnp.float32)[:, jnp.clip(delta + steps, 0, 2 * steps)]
    s = jnp.einsum('znqhd,znkhd->znhqk', qs, kw).astype(jnp.float32) * dh ** -0.5 + bias_m[None, None]
    s = jnp.where(mask[None, :, None], s, NEG_INF)
    lse = jax.nn.logsumexp(s, axis=-1)
    p = jnp.exp(s - lse[..., None])
    o = jnp.einsum('znhqk,znkhd->znqhd', p.astype(v.dtype), vw).reshape(Z, Mp, H, dh)[:, :M]
    lse = lse.transpose(0, 1, 3, 2).reshape(Z, Mp, H)[:, :M]

    def from_sub(t):
        return t.reshape(Bn, dil, M, *t.shape[2:]).swapaxes(1, 2).reshape(Bn, S, *t.shape[2:])

    return from_sub(o), from_sub(lse)


def dilated_mixture_attention(q, k, v, rel_bias):
    Bn, S, _ = q.shape
    Hg = D_HEADS_PER_GROUP
    shp = (Bn, S, len(D_GROUPS), Hg, D_HEAD_DIM)
    q = q.reshape(shp)
    k = k.reshape(shp)
    v = v.reshape(shp)
    outs = []
    lses = []
    for gi, (window, dil) in enumerate(D_GROUPS):
        steps = window // (2 * dil)
        buckets = t5_bucket(np.arange(-steps, steps + 1) * dil)
        bias = rel_bias[jnp.asarray(buckets)][:, gi * Hg:(gi + 1) * Hg].T
        o, lse = banded_dilated_attention(q[:, :, gi], k[:, :, gi], v[:, :, gi], bias, dil, steps)
        outs.append(o.astype(jnp.float32))
        lses.append(lse)
    wts = jax.nn.softmax(jnp.stack(lses, axis=0), axis=0)
    o = jnp.sum(wts[..., None] * jnp.stack(outs, axis=0), axis=0)
    return o.reshape(Bn, S, Hg * D_HEAD_DIM)


def _odd_mixer(h, w_in, w_out, conv_w, a_log, dt_bias, o_gain, rel_bias):
    Bn, S, _ = h.shape
    dt = h.dtype
    proj = h @ w_in
    qkv, z, beta_l, alpha_l, dq, dk, dv = _split(proj, (3 * C_WIDTH, C_WIDTH, 2 * C_HEADS, 2 * C_HEADS, D_WIDTH, D_WIDTH, D_WIDTH))
    qkv = jax.nn.silu(centred_dwconv(qkv.astype(jnp.float32), conv_w.astype(jnp.float32)))
    cq, ck, cv = [t.reshape(Bn, S, C_HEADS, C_HEAD_DIM) for t in jnp.split(qkv, 3, axis=-1)]
    cq = l2_norm(cq)
    ck = l2_norm(ck)
    beta = jax.nn.sigmoid(beta_l.astype(jnp.float32)).reshape(Bn, S, 2, C_HEADS)
    g = -jnp.exp(a_log.astype(jnp.float32)) * jax.nn.softplus(alpha_l.astype(jnp.float32).reshape(Bn, S, 2, C_HEADS) + dt_bias.astype(jnp.float32))
    o_f = gated_delta_chunked(cq, ck, cv, g[:, :, 0], beta[:, :, 0])
    rev = lambda t: jnp.flip(t, 1)
    o_r = rev(gated_delta_chunked(rev(cq), rev(ck), rev(cv), rev(g[:, :, 1]), rev(beta[:, :, 1])))
    o_c = rms_norm(o_f + o_r, o_gain) * jax.nn.silu(z.astype(jnp.float32).reshape(Bn, S, C_HEADS, C_HEAD_DIM))
    y_c = o_c.reshape(Bn, S, C_WIDTH).astype(dt)
    y_d = dilated_mixture_attention(dq, dk, dv, rel_bias).astype(dt)
    return jnp.concatenate([y_c, y_d], axis=-1) @ w_out


def _sq_relu_mlp(h, w1, w2):
    return jnp.square(jax.nn.relu(h @ w1)) @ w2


def _trunk(x, p):
    for layer in range(DEPTH):
        h = rms_norm(x, p['norm_mix'][layer])
        j = layer // 2
        if layer % 2 == 0:
            y = _even_mixer(h, p['w_in_e'][j], p['w_out_e'][j], p['a_qnorm'][j], p['a_knorm'][j],
                            p['b_conv_w'][j], p['b_conv_b'][j], p['b_wr'][j], p['b_br'][j],
                            p['b_wi'][j], p['b_bi'][j], p['b_lambda'][j])
        else:
            y = _odd_mixer(h, p['w_in_o'][j], p['w_out_o'][j], p['c_conv_w'][j], p['c_a_log'][j],
                           p['c_dt_bias'][j], p['c_norm'][j], p['rel_bias'])
        x = x + y.astype(x.dtype)
        h = rms_norm(x, p['norm_ff'][layer])
        x = x + _sq_relu_mlp(h, p['w_ff1'][layer], p['w_ff2'][layer]).astype(x.dtype)
    return rms_norm(x, p['norm_final'])


def setup_inputs(seed: int = 0) -> dict:
    key = jax.random.key(seed)
    ks = jax.random.split(key, 25)
    f32 = jnp.float32

    def nrm(k, shape, scale):
        return jax.random.normal(k, shape, f32) * scale

    def gain(k, shape):
        return 1.0 + 0.02 * jax.random.normal(k, shape, f32)

    x_prompt = nrm(ks[0], (BATCH, SEQ, D_MODEL), 1.0)
    x_sample = nrm(ks[1], (DEC_BATCH, DEC_SEQ, D_MODEL), 1.0)
    rel_bias = nrm(ks[2], (N_BUCKETS, D_NHEADS), 0.5)
    norm_mix = gain(ks[3], (DEPTH, D_MODEL))
    norm_ff = gain(ks[4], (DEPTH, D_MODEL))
    norm_final = gain(ks[5], (D_MODEL,))
    w_ff1 = nrm(ks[6], (DEPTH, D_MODEL, D_FF), D_MODEL ** -0.5)
    w_ff2 = nrm(ks[7], (DEPTH, D_FF, D_MODEL), D_FF ** -0.5)
    w_in_e = nrm(ks[8], (N_EVEN, D_MODEL, EVEN_IN), D_MODEL ** -0.5)
    w_out_e = nrm(ks[9], (N_EVEN, EVEN_OUT, D_MODEL), EVEN_OUT ** -0.5)
    a_qnorm = gain(ks[10], (N_EVEN, A_HEAD_DIM))
    a_knorm = gain(ks[11], (N_EVEN, A_HEAD_DIM))
    b_conv_w = nrm(ks[12], (N_EVEN, B_CONV, B_WIDTH), B_CONV ** -0.5)
    b_conv_b = nrm(ks[13], (N_EVEN, B_WIDTH), 0.02)
    b_wr = nrm(ks[14], (N_EVEN, 2, B_BLOCKS, B_BLOCK_DIM, B_BLOCK_DIM), B_BLOCK_DIM ** -0.5)
    b_br = nrm(ks[15], (N_EVEN, 2, B_WIDTH), 0.1)
    b_wi = nrm(ks[16], (N_EVEN, 2, B_BLOCKS, B_BLOCK_DIM, B_BLOCK_DIM), B_BLOCK_DIM ** -0.5)
    b_bi = nrm(ks[17], (N_EVEN, 2, B_WIDTH), 0.1)
    a0 = jax.random.uniform(ks[18], (N_EVEN, 2, B_WIDTH), f32, 0.9, 0.999)
    s0 = a0 ** (1.0 / RG_C)
    b_lambda = jnp.log(s0) - jnp.log1p(-s0)
    w_in_o = nrm(ks[19], (N_ODD, D_MODEL, ODD_IN), D_MODEL ** -0.5)
    w_out_o = nrm(ks[20], (N_ODD, ODD_OUT, D_MODEL), ODD_OUT ** -0.5)
    c_conv_w = nrm(ks[21], (N_ODD, C_CONV, 3 * C_WIDTH), C_CONV ** -0.5)
    c_a_log = jnp.log(jax.random.uniform(ks[22], (N_ODD, 2, C_HEADS), f32, 1.0, 16.0))
    dtv = jnp.exp(jax.random.uniform(ks[23], (N_ODD, 2, C_HEADS), f32, math.log(1e-3), math.log(1e-1)))
    c_dt_bias = dtv + jnp.log(-jnp.expm1(-dtv))
    c_norm = gain(ks[24], (N_ODD, C_HEAD_DIM))
    return {'x_prompt': x_prompt, 'x_sample': x_sample, 'rel_bias': rel_bias,
            'norm_mix': norm_mix, 'norm_ff': norm_ff, 'norm_final': norm_final,
            'w_ff1': w_ff1, 'w_ff2': w_ff2,
            'w_in_e': w_in_e, 'w_out_e': w_out_e, 'a_qnorm': a_qnorm, 'a_knorm': a_knorm,
            'b_conv_w': b_conv_w, 'b_conv_b': b_conv_b, 'b_wr': b_wr, 'b_br': b_br,
            'b_wi': b_wi, 'b_bi': b_bi, 'b_lambda': b_lambda,
            'w_in_o': w_in_o, 'w_out_o': w_out_o, 'c_conv_w': c_conv_w, 'c_a_log': c_a_log,
            'c_dt_bias': c_dt_bias, 'c_norm': c_norm}


def reference(x_prompt, x_sample, rel_bias, norm_mix, norm_ff, norm_final, w_ff1, w_ff2,
              w_in_e, w_out_e, a_qnorm, a_knorm, b_conv_w, b_conv_b, b_wr, b_br, b_wi, b_bi, b_lambda,
              w_in_o, w_out_o, c_conv_w, c_a_log, c_dt_bias, c_norm):
    params = dict(rel_bias=rel_bias, norm_mix=norm_mix, norm_ff=norm_ff, norm_final=norm_final,
                  w_ff1=w_ff1, w_ff2=w_ff2, w_in_e=w_in_e, w_out_e=w_out_e,
                  a_qnorm=a_qnorm, a_knorm=a_knorm, b_conv_w=b_conv_w, b_conv_b=b_conv_b,
                  b_wr=b_wr, b_br=b_br, b_wi=b_wi, b_bi=b_bi, b_lambda=b_lambda,
                  w_in_o=w_in_o, w_out_o=w_out_o, c_conv_w=c_conv_w, c_a_log=c_a_log,
                  c_dt_bias=c_dt_bias, c_norm=c_norm)
    y_prompt = _trunk(x_prompt, params)
    y_sample = _trunk(x_sample, params)
    return (y_prompt, y_sample)
```

```python
import math
from contextlib import ExitStack
import numpy as np
import concourse.bass as bass
import concourse.mybir as mybir
from concourse.bass_utils import run_bass_kernel_spmd

F32, BF16 = mybir.dt.float32, mybir.dt.bfloat16
ALU, AF, AX = mybir.AluOpType, mybir.ActivationFunctionType, mybir.AxisListType
D = 1024
EPS = 1e-6
NDS = 24
DEBUG = False
O_STOP = 99
O2B_LVL = 9


class Tile:
    __slots__ = ("h", "w", "r", "psum")

    def __init__(s, h, psum=False):
        s.h = h
        s.w = None
        s.r = {}
        s.psum = psum

    def __getitem__(s, k):
        return s.h[k]


class View:
    __slots__ = ("h", "par", "psum")

    def __init__(s, par, ap):
        s.h = ap
        s.par = par
        s.psum = par.psum

    def __getitem__(s, k):
        return s.h[k]

    w = property(lambda s: s.par.w, lambda s, v: setattr(s.par, 'w', v))
    r = property(lambda s: s.par.r, lambda s, v: setattr(s.par, 'r', v))


class KB:
    def __init__(s):
        s.nc = bass.Bass("TRN2", target_bir_lowering=False)
        nc = s.nc
        s.eng = {'pe': nc.tensor, 'act': nc.scalar, 'dve': nc.vector, 'pool': nc.gpsimd, 'sp': nc.sync}
        s.stack = ExitStack()
        s.sem = {e: s.stack.enter_context(nc.semaphore("c_" + e)) for e in ('pe', 'act', 'dve', 'pool')}
        s.cnt = {e: 0 for e in s.sem}
        s.waited = {e: {} for e in s.eng}
        s.dsem = [[s.stack.enter_context(nc.semaphore("d%d" % i)), 0] for i in range(NDS)]
        s.dnext = 0
        s.uid = 0
        s.ph = None

    def sb(s, shape, dt=F32, name="t"):
        s.uid += 1
        return Tile(s.ph.enter_context(s.nc.sbuf_tensor("%s_%d" % (name, s.uid), list(shape), dt)))

    def ps(s, shape, dt=F32, name="p"):
        s.uid += 1
        bank = 512 if dt == F32 else 1024
        h = s.ph.enter_context(s.nc.psum_tensor("%s_%d" % (name, s.uid), [128, bank], dt))
        n = 1
        for v in shape[1:]:
            n *= v
        assert n <= bank
        ap = h[0:shape[0], 0:n]
        if len(shape) == 3:
            ap = ap.rearrange("p (a b) -> p a b", b=shape[2])
        elif len(shape) == 4:
            ap = ap.rearrange("p (a b c) -> p a b c", b=shape[2], c=shape[3])
        return Tile(ap, psum=True)

    def dram(s, name, shape, dt=F32, kind="Internal"):
        if kind == "Internal" and DEBUG:
            kind = "ExternalOutput"
        return s.nc.dram_tensor(name, list(shape), dt, kind=kind).ap()

    def _wait(s, e, deps):
        eng = s.eng[e]
        wd = s.waited[e]
        for sem, val in deps.items():
            if wd.get(sem, 0) < val:
                eng.wait_ge(sem, val)
                wd[sem] = val

    @staticmethod
    def _deps(reads, writes):
        d = {}
        for t in reads:
            if t.w is not None and d.get(t.w[0], 0) < t.w[1]:
                d[t.w[0]] = t.w[1]
        for t in writes:
            if t.w is not None and d.get(t.w[0], 0) < t.w[1]:
                d[t.w[0]] = t.w[1]
            for sem, val in t.r.items():
                if d.get(sem, 0) < val:
                    d[sem] = val
        return d

    @staticmethod
    def _mark(tok, reads, writes):
        for t in reads:
            if t.r.get(tok[0], 0) < tok[1]:
                t.r[tok[0]] = tok[1]
        for t in writes:
            t.w = tok
            t.r = {}

    def op(s, e, fn, reads=(), writes=()):
        px = [t for t in reads if t.psum]
        if px:
            reads = [t for t in reads if not t.psum]
            writes = list(writes) + px
        d = s._deps(reads, writes)
        if e == 'pe':
            d.pop(s.sem['pe'], None)
        s._wait(e, d)
        ins = fn(s.eng[e])
        s.cnt[e] += 1
        ins.then_inc(s.sem[e], 1)
        s._mark((s.sem[e], s.cnt[e]), reads, writes)

    def dma(s, e, out, in_, reads=(), writes=(), slow=False):
        d = s._deps(reads, writes)
        slot = s.dsem[s.dnext]
        s.dnext = (s.dnext + 1) % len(s.dsem)
        sem, c = slot
        if c > 0:
            d[sem] = max(d.get(sem, 0), c)
        s._wait(e, d)
        if slow:
            s.eng[e].dma_start(out=out, in_=in_, allow_slow_non_contiguous=True).then_inc(sem, 16)
        else:
            s.eng[e].dma_start(out=out, in_=in_).then_inc(sem, 16)
        slot[1] = c + 16
        s._mark((sem, c + 16), reads, writes)

    def barrier(s):
        d = {s.sem[f]: s.cnt[f] for f in s.sem if s.cnt[f] > 0}
        for sem, c in s.dsem:
            if c > 0:
                d[sem] = c
        for e in s.eng:
            s._wait(e, dict(d))

    def mm(s, out_t, out_ap, lhsT_t, lhsT_ap, rhs_t, rhs_ap, start=True, stop=True):
        s.op('pe', lambda e: e.matmul(out_ap, lhsT=lhsT_ap, rhs=rhs_ap, start=start, stop=stop),
             reads=[lhsT_t, rhs_t] + ([] if start else [out_t]), writes=[out_t])

    def act(s, out_t, out_ap, in_t, in_ap, func, extra_reads=(), **kw):
        s.op('act', lambda e: e.activation(out=out_ap, in_=in_ap, func=func, **kw),
             reads=[in_t] + list(extra_reads), writes=[out_t])


class Rot:
    def __init__(s, tiles):
        s.t = tiles
        s.i = 0

    def next(s):
        t = s.t[s.i]
        s.i = (s.i + 1) % len(s.t)
        return t


def load_w_bf16(kb, w_ap, K, N, name):
    nch = K // 128
    t = kb.sb([128, nch, N], BF16, name)
    for c in range(nch):
        kb.dma('pool', t[:, c, :], w_ap[c * 128:(c + 1) * 128, :], writes=[t])
    return t


def load_col(kb, vec_ap, n, name):
    t = kb.sb([128, n], F32, name)
    kb.dma('sp', t[:, :], vec_ap.rearrange("(c p) -> p c", p=128), writes=[t], slow=True)
    return t


def load_bcast(kb, vec_ap, n, name):
    t = kb.sb([128, n], F32, name)
    kb.dma('sp', t[:, :], vec_ap.partition_broadcast(128), writes=[t])
    return t


class NormT:
    def __init__(s, kb, g_col, ident_bf):
        s.kb = kb
        s.g = g_col
        s.ident = ident_bf
        s.sq = Rot([kb.sb([128, D], BF16, "nsq") for _ in range(2)])
        s.ss = Rot([kb.sb([128, 1], F32, "nss") for _ in range(2)])
        s.rs = Rot([kb.sb([128, 1], F32, "nrs") for _ in range(2)])
        s.hb = Rot([kb.sb([128, D], BF16, "nhb") for _ in range(2)])
        s.pt = Rot([kb.ps([128, 8, 128], BF16, "npt") for _ in range(2)])

    def rstd(s, x_t, x_ap):
        kb = s.kb
        sq, ss, rs = s.sq.next(), s.ss.next(), s.rs.next()
        kb.op('act', lambda e: e.activation(out=sq[:, :], in_=x_ap, func=AF.Square, accum_out=ss[:, :]),
              reads=[x_t], writes=[sq, ss])
        kb.act(ss, ss[:, :], ss, ss[:, :], AF.Sqrt, scale=1.0 / D, bias=s.epsb[:, :], extra_reads=[s.epsb])
        kb.op('dve', lambda e: e.reciprocal(out=rs[:, :], in_=ss[:, :]), reads=[ss], writes=[rs])
        return rs

    def __call__(s, x_t, x_ap, out_t, out_ap):
        kb = s.kb
        rs = s.rstd(x_t, x_ap)
        hb, pt = s.hb.next(), s.pt.next()
        kb.op('dve', lambda e: e.tensor_scalar(out=hb[:, :], in0=x_ap, scalar1=rs[:, :], scalar2=None, op0=ALU.mult),
              reads=[x_t, rs], writes=[hb])
        for c in range(8):
            kb.op('pe', lambda e: e.transpose(pt[:, c, :], hb[:, c * 128:(c + 1) * 128], s.ident[:, :]),
                  reads=[hb, s.ident], writes=[pt])
        kb.op('dve', lambda e: e.tensor_tensor(out=out_ap, in0=pt[:, :, :],
                                               in1=s.g[:, :].unsqueeze(2).to_broadcast([128, 8, 128]), op=ALU.mult),
              reads=[pt, s.g], writes=[out_t])


def make_consts(kb, C):
    ident_bf = kb.sb([128, 128], BF16, "identb")
    kb.dma('pool', ident_bf[:, :], C['ident'], writes=[ident_bf])
    ident_f = kb.sb([128, 128], F32, "identf")
    kb.dma('sp', ident_f[:, :], C['ident'], writes=[ident_f])
    epsb = kb.sb([128, 1], F32, "epsb")
    kb.op('dve', lambda e: e.memset(epsb[:, :], EPS), writes=[epsb])
    return ident_bf, ident_f, epsb


def phase_mlp(kb, P, C, x_src, x_dst, layer, T, final_out=None):
    with ExitStack() as ph:
        kb.ph = ph
        ident_bf, ident_f, epsb = make_consts(kb, C)
        w1 = load_w_bf16(kb, P['w_ff1'][layer], D, 4 * D, "w1")
        w2 = load_w_bf16(kb, P['w_ff2'][layer], 4 * D, D, "w2")
        g = load_col(kb, P['norm_ff'][layer], 8, "gff")
        nt = NormT(kb, g, ident_bf)
        nt.epsb = epsb
        if final_out is not None:
            gfin = load_bcast(kb, P['norm_final'], D, "gfin")
        TT = 256
        xs = Rot([kb.sb([128, 2, D], F32, "x") for _ in range(2)])
        hT = Rot([kb.sb([128, 8, TT], BF16, "hT") for _ in range(2)])
        aT = kb.sb([128, 32, TT], BF16, "aT")
        rl = Rot([kb.sb([128, TT], F32, "rl") for _ in range(3)])
        ps1 = Rot([kb.ps([128, 512], F32, "ps1") for _ in range(3)])
        ps2 = Rot([kb.ps([128, 512], F32, "ps2") for _ in range(2)])
        for it in range(T // TT):
            x = xs.next()
            h = hT.next()
            r0 = it * TT
            kb.dma('sp', x[:, :, :], x_src[r0:r0 + TT, :].rearrange("(s p) d -> p s d", p=128), writes=[x])
            for sub in range(2):
                nt(x, x[:, sub, :], h, h[:, :, sub * 128:(sub + 1) * 128])
            for f in range(32):
                p1 = ps1.next()
                for c in range(8):
                    kb.mm(p1, p1[:, 0:TT], w1, w1[:, c, f * 128:(f + 1) * 128], h, h[:, c, :], start=(c == 0), stop=(c == 7))
                r = rl.next()
                kb.act(r, r[:, :], p1, p1[:, 0:TT], AF.Relu)
                kb.op('dve', lambda e: e.tensor_tensor(out=aT[:, f, :], in0=r[:, :], in1=r[:, :], op=ALU.mult),
                      reads=[r], writes=[aT])
            for sub in range(2):
                for half in range(2):
                    p2 = ps2.next()
                    for f in range(32):
                        kb.mm(p2, p2[:, :], aT, aT[:, f, sub * 128:(sub + 1) * 128], w2, w2[:, f, half * 512:(half + 1) * 512],
                              start=(f == 0), stop=(f == 31))
                    kb.op('dve', lambda e: e.tensor_tensor(out=x[:, sub, half * 512:(half + 1) * 512],
                                                           in0=x[:, sub, half * 512:(half + 1) * 512], in1=p2[:, :], op=ALU.add),
                          reads=[x, p2], writes=[x])
            if final_out is None:
                kb.dma('pool', x_dst[r0:r0 + TT, :].rearrange("(s p) d -> p s d", p=128), x[:, :, :], reads=[x])
            else:
                for sub in range(2):
                    rs = nt.rstd(x, x[:, sub, :])
                    kb.op('dve', lambda e: e.scalar_tensor_tensor(out=x[:, sub, :], in0=x[:, sub, :], scalar=rs[:, :],
                                                                  in1=gfin[:, :], op0=ALU.mult, op1=ALU.mult),
                          reads=[x, rs, gfin], writes=[x])
                kb.dma('pool', final_out[r0:r0 + TT, :].rearrange("(s p) d -> p s d", p=128), x[:, :, :], reads=[x])
        kb.barrier()
    kb.ph = None


def phase_e1(kb, P, C, S, x_src, j, layer, T):
    with ExitStack() as ph:
        kb.ph = ph
        ident_bf, ident_f, epsb = make_consts(kb, C)
        w = load_w_bf16(kb, P['w_in_e'][j], D, 1792, "wine")
        g = load_col(kb, P['norm_mix'][layer], 8, "gmix")
        gq = load_bcast(kb, P['a_qnorm'][j], 64, "gq")
        gk = load_bcast(kb, P['a_knorm'][j], 64, "gk")
        kb.op('dve', lambda e: e.tensor_scalar(out=gq[:, :], in0=gq[:, :], scalar1=0.125, scalar2=None, op0=ALU.mult),
              reads=[gq], writes=[gq])
        nt = NormT(kb, g, ident_bf)
        nt.epsb = epsb
        xs = Rot([kb.sb([128, 4, D], F32, "x") for _ in range(2)])
        hT = Rot([kb.sb([128, 8, 512], BF16, "hT") for _ in range(2)])
        cs = Rot([kb.sb([128, 4, 64], F32, "cs") for _ in range(2)])
        pq = Rot([kb.ps([128, 512], F32, "pq") for _ in range(2)])
        pkv = Rot([kb.ps([128, 256], F32, "pkv") for _ in range(1)])
        pf = Rot([kb.ps([128, 512], F32, "pf") for _ in range(2)])
        ptr = Rot([kb.ps([128, 6, 128], BF16, "ptr") for _ in range(1)])
        sq = Rot([kb.sb([128, 10, 64], F32, "sq") for _ in range(2)])
        ss = Rot([kb.sb([128, 10], F32, "ss") for _ in range(2)])
        rs = Rot([kb.sb([128, 10], F32, "rs") for _ in range(2)])
        qn = Rot([kb.sb([128, 10, 64], F32, "qn") for _ in range(2)])
        t1 = Rot([kb.sb([128, 10, 2, 16], F32, "t1") for _ in range(2)])
        t2 = Rot([kb.sb([128, 10, 2, 16], F32, "t2") for _ in range(2)])
        qr = Rot([kb.sb([128, 10, 64], F32, "qr") for _ in range(2)])
        qrb = Rot([kb.sb([128, 12, 64], BF16, "qrb") for _ in range(2)])
        qT = Rot([kb.sb([128, 6, 128], BF16, "qT") for _ in range(2)])
        va = Rot([kb.sb([128, 2, 65], BF16, "va") for _ in range(2)])
        for v in va.t:
            kb.op('dve', lambda e: e.memset(v[:, :, :], 1.0), writes=[v])
        xrt = Rot([kb.sb([128, 4, 512], F32, "xrt") for _ in range(2)])
        ggt = Rot([kb.sb([128, 4, 512], F32, "ggt") for _ in range(2)])
        for grp in range(T // 512):
            x, h, c_s = xs.next(), hT.next(), cs.next()
            r0 = grp * 512
            kb.dma('sp', x[:, :, :], x_src[r0:r0 + 512, :].rearrange("(s p) d -> p s d", p=128), writes=[x])
            kb.dma('sp', c_s[:, :, :], C['rope'][r0:r0 + 512, :].rearrange("(s p) d -> p s d", p=128), writes=[c_s])
            for sub in range(4):
                nt(x, x[:, sub, :], h, h[:, :, sub * 128:(sub + 1) * 128])
            for sub in range(4):
                p_q, p_kv = pq.next(), pkv.next()
                hs = slice(sub * 128, (sub + 1) * 128)
                for c in range(8):
                    kb.mm(p_q, p_q[:, :], h, h[:, c, hs], w, w[:, c, 0:512], start=(c == 0), stop=(c == 7))
                for c in range(8):
                    kb.mm(p_kv, p_kv[:, :], h, h[:, c, hs], w, w[:, c, 512:768], start=(c == 0), stop=(c == 7))
                s_q, s_s, r_s, q_n = sq.next(), ss.next(), rs.next(), qn.next()
                kb.act(s_q, s_q[:, 0:8, :], p_q, p_q[:, :].rearrange("p (h d) -> p h d", d=64), AF.Square)
                kb.act(s_q, s_q[:, 8:10, :], p_kv, p_kv[:, 0:128].rearrange("p (h d) -> p h d", d=64), AF.Square)
                kb.op('dve', lambda e: e.tensor_reduce(out=s_s[:, :], in_=s_q[:, :, :], axis=AX.X, op=ALU.add),
                      reads=[s_q], writes=[s_s])
                kb.act(s_s, s_s[:, :], s_s, s_s[:, :], AF.Sqrt, scale=1.0 / 64, bias=epsb[:, :], extra_reads=[epsb])
                kb.op('dve', lambda e: e.reciprocal(out=r_s[:, :], in_=s_s[:, :]), reads=[s_s], writes=[r_s])
                kb.op('dve', lambda e: e.tensor_tensor(out=q_n[:, 0:8, :], in0=p_q[:, :].rearrange("p (h d) -> p h d", d=64),
                                                       in1=r_s[:, 0:8].unsqueeze(2).to_broadcast([128, 8, 64]), op=ALU.mult),
                      reads=[p_q, r_s], writes=[q_n])
                kb.op('dve', lambda e: e.tensor_tensor(out=q_n[:, 8:10, :], in0=p_kv[:, 0:128].rearrange("p (h d) -> p h d", d=64),
                                                       in1=r_s[:, 8:10].unsqueeze(2).to_broadcast([128, 2, 64]), op=ALU.mult),
                      reads=[p_kv, r_s], writes=[q_n])
                kb.op('dve', lambda e: e.tensor_tensor(out=q_n[:, 0:8, :], in0=q_n[:, 0:8, :],
                                                       in1=gq[:, :].unsqueeze(1).to_broadcast([128, 8, 64]), op=ALU.mult),
                      reads=[q_n, gq], writes=[q_n])
                kb.op('dve', lambda e: e.tensor_tensor(out=q_n[:, 8:10, :], in0=q_n[:, 8:10, :],
                                                       in1=gk[:, :].unsqueeze(1).to_broadcast([128, 2, 64]), op=ALU.mult),
                      reads=[q_n, gk], writes=[q_n])
                qv = q_n[:, :, :].rearrange("p h (x a f) -> p h x a f", x=2, a=2)
                a_, b_ = qv[:, :, :, 0, :], qv[:, :, :, 1, :]
                cc = c_s[:, sub, 0:32].rearrange("p (x f) -> p x f", x=2).unsqueeze(1).to_broadcast([128, 10, 2, 16])
                sn = c_s[:, sub, 32:64].rearrange("p (x f) -> p x f", x=2).unsqueeze(1).to_broadcast([128, 10, 2, 16])
                q_r, t_1, t_2 = qr.next(), t1.next(), t2.next()
                qrv = q_r[:, :, :].rearrange("p h (x a f) -> p h x a f", x=2, a=2)
                kb.op('dve', lambda e: e.tensor_tensor(out=t_1[:, :, :, :], in0=a_, in1=cc, op=ALU.mult), reads=[q_n, c_s], writes=[t_1])
                kb.op('dve', lambda e: e.tensor_tensor(out=t_2[:, :, :, :], in0=b_, in1=sn, op=ALU.mult), reads=[q_n, c_s], writes=[t_2])
                kb.op('dve', lambda e: e.tensor_tensor(out=qrv[:, :, :, 0, :], in0=t_1[:, :, :, :], in1=t_2[:, :, :, :], op=ALU.subtract),
                      reads=[t_1, t_2], writes=[q_r])
                kb.op('dve', lambda e: e.tensor_tensor(out=t_1[:, :, :, :], in0=a_, in1=sn, op=ALU.mult), reads=[q_n, c_s], writes=[t_1])
                kb.op('dve', lambda e: e.tensor_tensor(out=t_2[:, :, :, :], in0=b_, in1=cc, op=ALU.mult), reads=[q_n, c_s], writes=[t_2])
                kb.op('dve', lambda e: e.tensor_tensor(out=qrv[:, :, :, 1, :], in0=t_1[:, :, :, :], in1=t_2[:, :, :, :], op=ALU.add),
                      reads=[t_1, t_2], writes=[q_r])
                q_b = qrb.next()
                kb.act(q_b, q_b[:, 0:8, :], q_r, q_r[:, 0:8, :], AF.Copy)
                kb.op('dve', lambda e: e.tensor_copy(out=q_b[:, 8:12, :].rearrange("p (g u) d -> p g u d", u=2),
                                                     in_=q_r[:, 8:10, :].unsqueeze(2).to_broadcast([128, 2, 2, 64])),
                      reads=[q_r], writes=[q_b])
                p_t = ptr.next()
                qbf = q_b[:, :, :].rearrange("p h d -> p (h d)")
                for c in range(6):
                    kb.op('pe', lambda e: e.transpose(p_t[:, c, :], qbf[:, c * 128:(c + 1) * 128], ident_bf[:, :]),
                          reads=[q_b, ident_bf], writes=[p_t])
                q_T = qT.next()
                kb.act(q_T, q_T[:, :, :], p_t, p_t[:, :, :], AF.Copy)
                tok = slice(r0 + sub * 128, r0 + (sub + 1) * 128)
                kb.dma('pool', S['QT'][:, tok].rearrange("(c p) t -> p c t", p=128), q_T[:, 0:4, :], reads=[q_T])
                kb.dma('pool', S['KTD'][:, tok].rearrange("(c p) t -> p c t", p=128), q_T[:, 4:6, :], reads=[q_T])
                v_a = va.next()
                kb.act(v_a, v_a[:, :, 0:64], p_kv, p_kv[:, 128:256].rearrange("p (h d) -> p h d", d=64), AF.Copy)
                kb.dma('pool', S['VA'][tok, :], v_a[:, :, :].rearrange("p h d -> p (h d)"), reads=[v_a])
            x_r, g_g = xrt.next(), ggt.next()
            for f in range(8):
                p_f = pf.next()
                for c in range(8):
                    kb.mm(p_f, p_f[:, :], w, w[:, c, 768 + f * 128:768 + (f + 1) * 128], h, h[:, c, :], start=(c == 0), stop=(c == 7))
                if f < 4:
                    kb.act(x_r, x_r[:, f, :], p_f, p_f[:, :], AF.Copy)
                else:
                    kb.act(g_g, g_g[:, f - 4, :], p_f, p_f[:, :], AF.Gelu_apprx_tanh)
            kb.dma('pool', S['XR'][:, r0:r0 + 512].rearrange("(c p) t -> p c t", p=128), x_r[:, :, :], reads=[x_r])
            kb.dma('pool', S['GG'][:, r0:r0 + 512].rearrange("(c p) t -> p c t", p=128), g_g[:, :, :], reads=[g_g])
        kb.barrier()
    kb.ph = None


def phase_e2(kb, P, C, S, j, seqs, rev):
    d = 1 if rev else 0
    with ExitStack() as ph:
        kb.ph = ph
        TC = 2048
        cw = kb.sb([128, 4, 4], F32, "cw")
        for jj in range(4):
            kb.dma('sp', cw[:, jj, :], P['b_conv_w'][j, jj].rearrange("(c p) -> p c", p=128), writes=[cw], slow=True)
        cb = load_col(kb, P['b_conv_b'][j], 4, "cb")
        br = load_col(kb, P['b_br'][j, d], 4, "br")
        bi = load_col(kb, P['b_bi'][j, d], 4, "bi")
        lam = load_col(kb, P['b_lambda'][j, d], 4, "lam")
        c1 = kb.sb([128, 4], F32, "c1")
        kb.act(c1, c1[:, :], lam, lam[:, :], AF.Exp, scale=-1.0)
        kb.act(c1, c1[:, :], c1, c1[:, :], AF.Ln, bias=1.0)
        kb.op('dve', lambda e: e.tensor_scalar(out=c1[:, :], in0=c1[:, :], scalar1=-8.0, scalar2=None, op0=ALU.mult),
              reads=[c1], writes=[c1])
        wr = kb.sb([128, 4, 128], F32, "wr")
        wi = kb.sb([128, 4, 128], F32, "wi")
        for wt, src in ((wr, P['b_wr']), (wi, P['b_wi'])):
            kb.op('dve', lambda e: e.memset(wt[:, :, :], 0.0), writes=[wt])
            for cc in range(4):
                for b in range(2):
                    kb.dma('sp', wt[b * 64:(b + 1) * 64, cc, b * 64:(b + 1) * 64], src[j, d, 2 * cc + b], writes=[wt])
        xrt = Rot([kb.sb([128, TC + 3], F32, "xrt") for _ in range(2)])
        xc = Rot([kb.sb([128, TC], F32, "xc") for _ in range(2)])
        ga = Rot([kb.sb([128, TC], F32, "ga") for _ in range(2)])
        gm = Rot([kb.sb([128, TC], F32, "gm") for _ in range(2)])
        gu = Rot([kb.sb([128, TC], F32, "gu") for _ in range(2)])
        hh = Rot([kb.sb([128, TC], F32, "hh") for _ in range(2)])
        hf = Rot([kb.sb([128, TC], F32, "hf") for _ in range(2)])
        gg = Rot([kb.sb([128, TC], F32, "gg") for _ in range(2)])
        yb = Rot([kb.sb([128, TC], BF16, "yb") for _ in range(2)])
        carry = Rot([kb.sb([128, 1], F32, "carry") for _ in range(2)])
        pg = Rot([kb.ps([128, 512], F32, "pg") for _ in range(4)])
        t0 = 0
        for Sq in seqs:
            tc = min(TC, Sq)
            nchunk = Sq // tc
            for cc in range(4):
                rows = slice(cc * 128, (cc + 1) * 128)
                cr = None
                order = range(nchunk - 1, -1, -1) if rev else range(nchunk)
                for ic in order:
                    a0 = t0 + ic * tc
                    x_r = xrt.next()
                    lo = 2 if ic > 0 else 0
                    hi = 1 if ic < nchunk - 1 else 0
                    if lo == 0:
                        kb.op('dve', lambda e: e.memset(x_r[:, 0:2], 0.0), writes=[x_r])
                    if hi == 0:
                        kb.op('dve', lambda e: e.memset(x_r[:, tc + 2:tc + 3], 0.0), writes=[x_r])
                    kb.dma('sp', x_r[:, 2 - lo:tc + 2 + hi], S['XR'][rows, a0 - lo:a0 + tc + hi], writes=[x_r])
                    x_c = xc.next()
                    kb.op('dve', lambda e: e.tensor_scalar(out=x_c[:, 0:tc], in0=x_r[:, 0:tc], scalar1=cw[:, 0, cc:cc + 1],
                                                           scalar2=cb[:, cc:cc + 1], op0=ALU.mult, op1=ALU.add),
                          reads=[x_r, cw, cb], writes=[x_c])
                    for jj in range(1, 4):
                        kb.op('dve', lambda e: e.scalar_tensor_tensor(out=x_c[:, 0:tc], in0=x_r[:, jj:jj + tc], scalar=cw[:, jj, cc:cc + 1],
                                                                      in1=x_c[:, 0:tc], op0=ALU.mult, op1=ALU.add),
                              reads=[x_r, cw, x_c], writes=[x_c])
                    g_a, g_m, g_u = ga.next(), gm.next(), gu.next()
                    for n in range(tc // 512):
                        cols = slice(n * 512, (n + 1) * 512)
                        p_r, p_i = pg.next(), pg.next()
                        kb.mm(p_r, p_r[:, :], wr, wr[:, cc, :], x_c, x_c[:, cols])
                        kb.mm(p_i, p_i[:, :], wi, wi[:, cc, :], x_c, x_c[:, cols])
                        kb.act(g_a, g_a[:, cols], p_r, p_r[:, :], AF.Sigmoid, bias=br[:, cc:cc + 1], extra_reads=[br])
                        kb.act(g_u, g_u[:, cols], p_i, p_i[:, :], AF.Sigmoid, bias=bi[:, cc:cc + 1], extra_reads=[bi])
                    kb.act(g_a, g_a[:, 0:tc], g_a, g_a[:, 0:tc], AF.Exp, scale=c1[:, cc:cc + 1], extra_reads=[c1])
                    kb.act(g_m, g_m[:, 0:tc], g_a, g_a[:, 0:tc], AF.Square)
                    kb.act(g_m, g_m[:, 0:tc], g_m, g_m[:, 0:tc], AF.Sqrt, scale=-1.0, bias=1.0)
                    kb.op('dve', lambda e: e.tensor_tensor(out=g_u[:, 0:tc], in0=g_u[:, 0:tc], in1=x_c[:, 0:tc], op=ALU.mult),
                          reads=[g_u, x_c], writes=[g_u])
                    kb.op('dve', lambda e: e.tensor_tensor(out=g_u[:, 0:tc], in0=g_u[:, 0:tc], in1=g_m[:, 0:tc], op=ALU.mult),
                          reads=[g_u, g_m], writes=[g_u])
                    h_ = hh.next()
                    init = 0.0 if cr is None else cr[:, 0:1]
                    rd = [g_a, g_u] + ([] if cr is None else [cr])
                    if rev:
                        kb.op('dve', lambda e: e.tensor_tensor_scan(out=h_[:, tc - 1::-1] if False else h_[:, 0:tc][:, ::-1],
                                                                    data0=g_a[:, 0:tc][:, ::-1], data1=g_u[:, 0:tc][:, ::-1],
                                                                    initial=init, op0=ALU.mult, op1=ALU.add), reads=rd, writes=[h_])
                    else:
                        kb.op('dve', lambda e: e.tensor_tensor_scan(out=h_[:, 0:tc], data0=g_a[:, 0:tc], data1=g_u[:, 0:tc],
                                                                    initial=init, op0=ALU.mult, op1=ALU.add), reads=rd, writes=[h_])
                    cr = carry.next()
                    edge = 0 if rev else tc - 1
                    kb.op('dve', lambda e: e.tensor_copy(out=cr[:, :], in_=h_[:, edge:edge + 1]), reads=[h_], writes=[cr])
                    if not rev:
                        kb.dma('pool', S['HF'][rows, a0:a0 + tc], h_[:, 0:tc], reads=[h_])
                    else:
                        h_f, g_g, y_b = hf.next(), gg.next(), yb.next()
                        kb.dma('sp', h_f[:, 0:tc], S['HF'][rows, a0:a0 + tc], writes=[h_f])
                        kb.dma('sp', g_g[:, 0:tc], S['GG'][rows, a0:a0 + tc], writes=[g_g])
                        kb.op('dve', lambda e: e.tensor_tensor(out=h_[:, 0:tc], in0=h_[:, 0:tc], in1=h_f[:, 0:tc], op=ALU.add),
                              reads=[h_, h_f], writes=[h_])
                        kb.op('dve', lambda e: e.tensor_tensor(out=y_b[:, 0:tc], in0=h_[:, 0:tc], in1=g_g[:, 0:tc], op=ALU.mult),
                              reads=[h_, g_g], writes=[y_b])
                        kb.dma('pool', S['YB'][rows, a0:a0 + tc], y_b[:, 0:tc], reads=[y_b])
            t0 += Sq
        kb.barrier()
    kb.ph = None


def phase_e3(kb, P, C, S, seqs):
    with ExitStack() as ph:
        kb.ph = ph
        Smax = max(seqs)
        kt_t = kb.sb([128, 2, Smax], BF16, "kt")
        va_t = kb.sb([128, Smax // 128, 130], BF16, "vat")
        ones = kb.sb([128, 64], F32, "ones")
        kb.op('dve', lambda e: e.memset(ones[:, :], 1.0), writes=[ones])
        qt = Rot([kb.sb([128, 4, 512], BF16, "qt") for _ in range(2)])
        pe_ = Rot([kb.sb([128, 512], BF16, "pexp") for _ in range(4)])
        num = Rot([kb.sb([65, 512], F32, "num") for _ in range(2)])
        rec = Rot([kb.sb([64, 512], F32, "rec") for _ in range(2)])
        yt = Rot([kb.sb([64, 512], BF16, "yt") for _ in range(3)])
        pss = Rot([kb.ps([128, 512], F32, "pss") for _ in range(4)])
        pac = Rot([kb.ps([65, 512], F32, "pac") for _ in range(2)])
        pdn = Rot([kb.ps([64, 512], F32, "pdn") for _ in range(1)])
        t0 = 0
        for Sq in seqs:
            nkt = Sq // 128
            for c0 in range(0, Sq, 1024):
                for g_ in range(2):
                    kb.dma('sp', kt_t[:, g_, c0:c0 + 1024], S['KTD'][g_ * 128:(g_ + 1) * 128, t0 + c0:t0 + c0 + 1024], writes=[kt_t])
                kb.dma('sp', va_t[:, c0 // 128:c0 // 128 + 8, :], S['VA'][t0 + c0:t0 + c0 + 1024, :].rearrange("(k p) f -> p k f", p=128), writes=[va_t])
            for qi in range(Sq // 512):
                q0 = t0 + qi * 512
                q_t = qt.next()
                kb.dma('sp', q_t[:, :, :], S['QT'][:, q0:q0 + 512].rearrange("(c p) t -> p c t", p=128), writes=[q_t])
                for h in range(8):
                    ch, pb, g = h // 2, (h % 2) * 64, h // 4
                    acc = pac.next()
                    for k in range(nkt):
                        p_s = pss.next()
                        kb.mm(p_s, p_s[:, :], kt_t, kt_t[pb:pb + 64, g, k * 128:(k + 1) * 128], q_t, q_t[pb:pb + 64, ch, :])
                        p_e = pe_.next()
                        kb.act(p_e, p_e[:, :], p_s, p_s[:, :], AF.Exp)
                        kb.mm(acc, acc[:, :], va_t, va_t[:, k, g * 65:(g + 1) * 65], p_e, p_e[:, :], start=(k == 0), stop=(k == nkt - 1))
                    n_, r_, y_, p_d = num.next(), rec.next(), yt.next(), pdn.next()
                    kb.act(n_, n_[:, :], acc, acc[:, :], AF.Copy)
                    kb.mm(p_d, p_d[:, :], ones, ones[64:65, 0:64], n_, n_[64:65, :])
                    kb.op('dve', lambda e: e.reciprocal(out=r_[:, :], in_=p_d[:, :]), reads=[p_d], writes=[r_])
                    kb.op('dve', lambda e: e.tensor_tensor(out=y_[:, :], in0=n_[0:64, :], in1=r_[:, :], op=ALU.mult),
                          reads=[n_, r_], writes=[y_])
                    kb.dma('pool', S['YA'][h * 64:(h + 1) * 64, q0:q0 + 512], y_[:, :], reads=[y_])
            t0 += Sq
        kb.barrier()
    kb.ph = None


def phase_outproj(kb, P, C, S, w_ap, srcs, x_src, x_dst, T):
    with ExitStack() as ph:
        kb.ph = ph
        nch = sum(n for _, n in srcs)
        w = load_w_bf16(kb, w_ap, nch * 128, D, "wout")
        xs = Rot([kb.sb([128, 4, D], F32, "x") for _ in range(2)])
        yT = Rot([kb.sb([128, nch, 512], BF16, "yT") for _ in range(2)])
        pso = Rot([kb.ps([128, 512], F32, "pso") for _ in range(4)])
        for grp in range(T // 512):
            r0 = grp * 512
            x, y_ = xs.next(), yT.next()
            kb.dma('sp', x[:, :, :], x_src[r0:r0 + 512, :].rearrange("(s p) d -> p s d", p=128), writes=[x])
            c0 = 0
            for ap_, n in srcs:
                kb.dma('sp', y_[:, c0:c0 + n, :], ap_[:, r0:r0 + 512].rearrange("(c p) t -> p c t", p=128), writes=[y_])
                c0 += n
            for sub in range(4):
                for half in range(2):
                    p_o = pso.next()
                    for c in range(nch):
                        kb.mm(p_o, p_o[:, :], y_, y_[:, c, sub * 128:(sub + 1) * 128], w, w[:, c, half * 512:(half + 1) * 512],
                              start=(c == 0), stop=(c == nch - 1))
                    kb.op('dve', lambda e: e.tensor_tensor(out=x[:, sub, half * 512:(half + 1) * 512],
                                                           in0=x[:, sub, half * 512:(half + 1) * 512], in1=p_o[:, :], op=ALU.add),
                          reads=[x, p_o], writes=[x])
            kb.dma('pool', x_dst[r0:r0 + 512, :].rearrange("(s p) d -> p s d", p=128), x[:, :, :], reads=[x])
        kb.barrier()
    kb.ph = None


def phase_o1(kb, P, C, S, x_src, j, layer, T):
    with ExitStack() as ph:
        kb.ph = ph
        ident_bf, ident_f, epsb = make_consts(kb, C)
        w = load_w_bf16(kb, P['w_in_o'][j], D, 4368, "wino")
        g = load_col(kb, P['norm_mix'][layer], 8, "gmix")
        nexpa = load_bcast(kb, P['c_a_log'][j].rearrange("a b -> (a b)"), 8, "nexpa")
        dtb = load_bcast(kb, P['c_dt_bias'][j].rearrange("a b -> (a b)"), 8, "dtb")
        kb.act(nexpa, nexpa[:, :], nexpa, nexpa[:, :], AF.Exp)
        kb.op('dve', lambda e: e.tensor_scalar(out=nexpa[:, :], in0=nexpa[:, :], scalar1=-1.0, scalar2=None, op0=ALU.mult),
              reads=[nexpa], writes=[nexpa])
        nt = NormT(kb, g, ident_bf)
        nt.epsb = epsb
        xs = Rot([kb.sb([128, 4, D], F32, "x") for _ in range(2)])
        hT = Rot([kb.sb([128, 8, 512], BF16, "hT") for _ in range(2)])
        pf = Rot([kb.ps([128, 512], F32, "pf") for _ in range(4)])
        pb = Rot([kb.ps([128, 16], F32, "pb") for _ in range(1)])
        qkv = Rot([kb.sb([128, 12, 512], F32, "qkv") for _ in range(1)])
        dqk = Rot([kb.sb([128, 12, 512], BF16, "dqk") for _ in range(1)])
        zs = Rot([kb.sb([128, 4, 512], F32, "zs") for _ in range(1)])
        bg = Rot([kb.sb([128, 4, 16], F32, "bg") for _ in range(2)])
        tmp = Rot([kb.sb([128, 8], F32, "tmp") for _ in range(2)])
        dv = Rot([kb.sb([128, 12, 65], BF16, "dv") for _ in range(2)])
        for v in dv.t:
            kb.op('dve', lambda e: e.memset(v[:, :, :], 1.0), writes=[v])
        for grp in range(T // 512):
            x, h = xs.next(), hT.next()
            r0 = grp * 512
            kb.dma('sp', x[:, :, :], x_src[r0:r0 + 512, :].rearrange("(s p) d -> p s d", p=128), writes=[x])
            for sub in range(4):
                nt(x, x[:, sub, :], h, h[:, :, sub * 128:(sub + 1) * 128])
            q_ = qkv.next()
            for f in range(12):
                p_f = pf.next()
                for c in range(8):
                    kb.mm(p_f, p_f[:, :], w, w[:, c, f * 128:(f + 1) * 128], h, h[:, c, :], start=(c == 0), stop=(c == 7))
                kb.act(q_, q_[:, f, :], p_f, p_f[:, :], AF.Copy)
            kb.dma('pool', S['QKV'][:, r0:r0 + 512].rearrange("(c p) t -> p c t", p=128), q_[:, :, :], reads=[q_])
            d_ = dqk.next()
            for f in range(12):
                p_f = pf.next()
                c0 = 2064 + f * 128
                for c in range(8):
                    kb.mm(p_f, p_f[:, :], w, w[:, c, c0:c0 + 128], h, h[:, c, :], start=(c == 0), stop=(c == 7))
                kb.act(d_, d_[:, f, :], p_f, p_f[:, :], AF.Copy, scale=(0.125 if f < 6 else 1.0))
            kb.dma('pool', S['DQK'][:, r0:r0 + 512].rearrange("(c p) t -> p c t", p=128), d_[:, :, :], reads=[d_])
            z_, b_ = zs.next(), bg.next()
            for sub in range(4):
                hs = slice(sub * 128, (sub + 1) * 128)
                p_f = pf.next()
                for c in range(8):
                    kb.mm(p_f, p_f[:, :], h, h[:, c, hs], w, w[:, c, 1536:2048], start=(c == 0), stop=(c == 7))
                kb.act(z_, z_[:, sub, :], p_f, p_f[:, :], AF.Silu)
                p_b = pb.next()
                for c in range(8):
                    kb.mm(p_b, p_b[:, :], h, h[:, c, hs], w, w[:, c, 2048:2064], start=(c == 0), stop=(c == 7))
                kb.act(b_, b_[:, sub, 0:8], p_b, p_b[:, 0:8], AF.Sigmoid)
                t_ = tmp.next()
                kb.op('dve', lambda e: e.tensor_tensor(out=t_[:, :], in0=p_b[:, 8:16], in1=dtb[:, :], op=ALU.add),
                      reads=[p_b, dtb], writes=[t_])
                kb.act(t_, t_[:, :], t_, t_[:, :], AF.Exp)
                kb.act(t_, t_[:, :], t_, t_[:, :], AF.Ln, bias=1.0)
                kb.op('dve', lambda e: e.tensor_tensor(out=b_[:, sub, 8:16], in0=t_[:, :], in1=nexpa[:, :], op=ALU.mult),
                      reads=[t_, nexpa], writes=[b_])
                v_ = dv.next()
                for part, (a, n) in enumerate(((3600, 512), (4112, 256))):
                    p_f = pf.next()
                    for c in range(8):
                        kb.mm(p_f, p_f[:, 0:n], h, h[:, c, hs], w, w[:, c, a:a + n], start=(c == 0), stop=(c == 7))
                    nh = n // 64
                    kb.act(v_, v_[:, part * 8:part * 8 + nh, 0:64], p_f, p_f[:, 0:n].rearrange("p (h d) -> p h d", d=64), AF.Copy)
                tok = slice(r0 + sub * 128, r0 + (sub + 1) * 128)
                kb.dma('pool', S['DV'][tok, :], v_[:, :, :].rearrange("p h d -> p (h d)"), reads=[v_])
            kb.dma('pool', S['ZS'][r0:r0 + 512, :].rearrange("(s p) d -> p s d", p=128), z_[:, :, :], reads=[z_])
            kb.dma('pool', S['BG'][r0:r0 + 512, :].rearrange("(s p) d -> p s d", p=128), b_[:, :, :], reads=[b_])
        kb.barrier()
    kb.ph = None


def phase_o2a(kb, P, C, S, j, seqs):
    with ExitStack() as ph:
        kb.ph = ph
        ident_bf, ident_f, epsb = make_consts(kb, C)
        cw = kb.sb([128, 4, 12], F32, "cw")
        for jj in range(4):
            kb.dma('sp', cw[:, jj, :], P['c_conv_w'][j, jj].rearrange("(c p) -> p c", p=128), writes=[cw], slow=True)
        ones = kb.sb([128, 128], F32, "ones")
        kb.op('dve', lambda e: e.memset(ones[:, :], 1.0), writes=[ones])
        xin = Rot([kb.sb([128, 12, 515], F32, "xin") for _ in range(2)])
        xc = Rot([kb.sb([128, 12, 512], F32, "xc") for _ in range(2)])
        sq = Rot([kb.sb([128, 512], F32, "sq") for _ in range(2)])
        rs = Rot([kb.sb([128, 512], F32, "rs") for _ in range(2)])
        kv = Rot([kb.sb([128, 4, 4, 256], F32, "kv") for _ in range(1)])
        pn = Rot([kb.ps([128, 512], F32, "pn") for _ in range(2)])
        pt = Rot([kb.ps([128, 4, 128], F32, "pt") for _ in range(2)])
        t0 = 0
        for Sq in seqs:
            for blk in range(Sq // 512):
                a0 = t0 + blk * 512
                x_ = xin.next()
                lo = 2 if blk > 0 else 0
                hi = 1 if blk < Sq // 512 - 1 else 0
                if lo == 0:
                    kb.op('dve', lambda e: e.memset(x_[:, :, 0:2], 0.0), writes=[x_])
                if hi == 0:
                    kb.op('dve', lambda e: e.memset(x_[:, :, 514:515], 0.0), writes=[x_])
                kb.dma('sp', x_[:, :, 2 - lo:514 + hi], S['QKV'][:, a0 - lo:a0 + 512 + hi].rearrange("(c p) t -> p c t", p=128), writes=[x_])
                c_ = xc.next()
                for f in range(12):
                    kb.op('dve', lambda e: e.tensor_scalar(out=c_[:, f, :], in0=x_[:, f, 0:512], scalar1=cw[:, 0, f:f + 1], scalar2=None, op0=ALU.mult),
                          reads=[x_, cw], writes=[c_])
                    for jj in range(1, 4):
                        kb.op('dve', lambda e: e.scalar_tensor_tensor(out=c_[:, f, :], in0=x_[:, f, jj:jj + 512], scalar=cw[:, jj, f:f + 1],
                                                                      in1=c_[:, f, :], op0=ALU.mult, op1=ALU.add),
                              reads=[x_, cw, c_], writes=[c_])
                kb.act(c_, c_[:, :, :], c_, c_[:, :, :], AF.Silu)
                for f in range(8):
                    s_, r_, p_n = sq.next(), rs.next(), pn.next()
                    kb.act(s_, s_[:, :], c_, c_[:, f, :], AF.Square)
                    kb.mm(p_n, p_n[:, :], ones, ones[:, :], s_, s_[:, :])
                    kb.act(s_, s_[:, :], p_n, p_n[:, :], AF.Sqrt, bias=epsb[:, :], extra_reads=[epsb])
                    kb.op('dve', lambda e: e.reciprocal(out=r_[:, :], in_=s_[:, :]), reads=[s_], writes=[r_])
                    if f < 4:
                        kb.op('dve', lambda e: e.scalar_tensor_tensor(out=c_[:, f, :], in0=c_[:, f, :], scalar=float(128 ** -0.5), in1=r_[:, :],
                                                                      op0=ALU.mult, op1=ALU.mult), reads=[c_, r_], writes=[c_])
                    else:
                        kb.op('dve', lambda e: e.tensor_tensor(out=c_[:, f, :], in0=c_[:, f, :], in1=r_[:, :], op=ALU.mult),
                              reads=[c_, r_], writes=[c_])
                kb.dma('pool', S['CQK'][:, a0:a0 + 512].rearrange("(c p) t -> p c t", p=128), c_[:, 0:8, :], reads=[c_])
                k_ = kv.next()
                for f in range(4, 12):
                    hd, kvi = (f - 4) % 4, (f - 4) // 4
                    p_t = pt.next()
                    for sub in range(4):
                        kb.op('pe', lambda e: e.transpose(p_t[:, sub, :], c_[:, f, sub * 128:(sub + 1) * 128], ident_f[:, :]),
                              reads=[c_, ident_f], writes=[p_t])
                    kb.act(k_, k_[:, :, hd, kvi * 128:(kvi + 1) * 128], p_t, p_t[:, :, :], AF.Copy)
                kb.dma('pool', S['CKV'][a0:a0 + 512, :].rearrange("(s p) f -> p s f", p=128), k_[:, :, :, :].rearrange("p s h f -> p s (h f)"), reads=[k_])
            t0 += Sq
        kb.barrier()
    kb.ph = None


def phase_o2b(kb, P, C, S, seqs, rev):
    d = 1 if rev else 0
    with ExitStack() as ph:
        kb.ph = ph
        ident_bf, ident_f, epsb = make_consts(kb, C)
        BA = kb.sb([128, 128], F32, "BA")
        MS = kb.sb([128, 128], F32, "MS")
        kb.dma('sp', BA[:, :], C['ba'][d], writes=[BA])
        kb.dma('sp', MS[:, :], C['ms'][d], writes=[MS])
        ones = kb.sb([128, 128], F32, "ones")
        kb.op('dve', lambda e: e.memset(ones[:, :], 1.0), writes=[ones])
        St = [kb.sb([128, 128], F32, "state") for _ in range(4)]
        qk = Rot([kb.sb([128, 8, 128], F32, "qk") for _ in range(2)])
        kvt = Rot([kb.sb([128, 4, 256], F32, "kvt") for _ in range(2)])
        bgt = Rot([kb.sb([128, 16], F32, "bgt") for _ in range(2)])
        sc = Rot([kb.sb([128, 5, 4], F32, "sc") for _ in range(2)])
        ot = Rot([kb.sb([128, 4, 128], F32, "ot") for _ in range(2)])
        oft = Rot([kb.sb([128, 4, 128], F32, "oft") for _ in range(2)])
        NB = 6
        pbank = [kb.ps([128, 4, 128], F32, "pb") for _ in range(NB)]
        pp = Rot([View(b, b[:, k, :]) for k in range(4) for b in pbank])
        pg = Rot([kb.ps([128, 16], F32, "pg") for _ in range(2)])

        def mk(name, n):
            return [Rot([kb.sb([128, 128], F32, name) for _ in range(n)]) for _ in range(4)]
        gY, Dm, DTm, Pa, Pta, Rr, ru, rw, uu, wT, AT, vn, o2, Kp = [mk(nm, 2) for nm in
            ("gY", "Dm", "DTm", "Pa", "Pta", "R", "ru", "rw", "u", "wT", "AT", "vn", "o2", "Kp")]
        extra_names = ("Rt", "M0", "M0t", "X", "Xt", "C1", "C1t", "C2", "C2t", "C3", "C3t", "C4t")
        extra = {nm: mk(nm, 2) for nm in extra_names}
        KM = kb.sb([128, 5, 128], F32, "KM")
        kb.dma('sp', KM[:, :, :], C['km'].rearrange("k p f -> p k f"), writes=[KM])
        t0 = 0
        for Sq in seqs:
            nch = Sq // 128
            for hd in range(4):
                kb.op('dve', lambda e: e.memset(St[hd][:, :], 0.0), writes=[St[hd]])
            for ic in (range(nch - 1, -1, -1) if rev else range(nch)):
                a0 = t0 + ic * 128
                q_, k_, b_, s_ = qk.next(), kvt.next(), bgt.next(), sc.next()
                kb.dma('sp', q_[:, :, :], S['CQK'][:, a0:a0 + 128].rearrange("(c p) t -> p c t", p=128), writes=[q_])
                kb.dma('sp', k_[:, :, :], S['CKV'][a0:a0 + 128, :].rearrange("p (h f) -> p h f", f=256), writes=[k_])
                kb.dma('sp', b_[:, :], S['BG'][a0:a0 + 128, :], writes=[b_])
                beta = b_[:, d * 4:d * 4 + 4]
                gcol = b_[:, 8 + d * 4:8 + d * 4 + 4]
                p_g = pg.next()
                kb.mm(p_g, p_g[:, 0:4], BA, BA[:, :], b_, gcol)
                kb.mm(p_g, p_g[:, 4:8], ones, ones[:, :], b_, gcol)
                kb.op('dve', lambda e: e.tensor_scalar(out=s_[:, 0, :], in0=beta, scalar1=-1.0, scalar2=None, op0=ALU.mult), reads=[b_], writes=[s_])
                kb.act(s_, s_[:, 1, :], p_g, p_g[:, 0:4], AF.Exp)
                kb.op('dve', lambda e: e.tensor_tensor(out=s_[:, 2, :], in0=p_g[:, 4:8], in1=p_g[:, 0:4], op=ALU.subtract) if False else
                      e.tensor_copy(out=s_[:, 2, :], in_=p_g[:, 0:4]), reads=[p_g], writes=[s_])
                kb.op('dve', lambda e: e.tensor_tensor(out=s_[:, 2, :], in0=p_g[:, 4:8], in1=s_[:, 2, :], op=ALU.subtract), reads=[p_g, s_], writes=[s_])
                kb.act(s_, s_[:, 2, :], s_, s_[:, 2, :], AF.Exp)
                kb.act(s_, s_[:, 3, :], p_g, p_g[:, 4:8], AF.Exp)
                kb.op('dve', lambda e: e.tensor_tensor(out=s_[:, 4, :], in0=beta, in1=s_[:, 1, :], op=ALU.mult), reads=[b_, s_], writes=[s_])
                o_ = ot.next()
                if O2B_LVL < 2:
                    kb.op('dve', lambda e: e.memset(o_[:, :, :], 0.0), writes=[o_])
                    kb.op('dve', lambda e: e.tensor_copy(out=o_[:, 0, 0:20], in_=s_[:, :, :].rearrange("p a b -> p (a b)")), reads=[s_], writes=[o_])
                    kb.dma('pool', S['OF'][a0:a0 + 128, :], o_[:, :, :].rearrange("p h f -> p (h f)"), reads=[o_])
                    continue
                T_ = {}
                for hd in range(4):
                    T_[hd] = dict(gY=gY[hd].next(), Dm=Dm[hd].next(), DTm=DTm[hd].next(), P=Pa[hd].next(), Pt=Pta[hd].next(),
                                  R=Rr[hd].next(), ru=ru[hd].next(), rw=rw[hd].next(),
                                  u=uu[hd].next(), wT=wT[hd].next(), AT=AT[hd].next(), vn=vn[hd].next(), o2=o2[hd].next(), Kp=Kp[hd].next())
                    for nm in extra_names:
                        T_[hd][nm] = extra[nm][hd].next()
                for hd in range(4):
                    t = T_[hd]
                    kT = q_[:, 4 + hd, :]
                    kb.op('dve', lambda e: e.tensor_scalar(out=t['gY'][:, :], in0=MS[:, :], scalar1=b_[:, 8 + d * 4 + hd:9 + d * 4 + hd], scalar2=None, op0=ALU.mult),
                          reads=[MS, b_], writes=[t['gY']])
                    p1, p2, p3 = pp.next(), pp.next(), pp.next()
                    kb.mm(p1, p1[:, :], BA, BA[:, :], t['gY'], t['gY'][:, :])
                    kb.mm(p2, p2[:, :], t['gY'], t['gY'][:, :], BA, BA[:, :])
                    kb.mm(p3, p3[:, :], q_, kT, q_, kT)
                    kb.act(t['Dm'], t['Dm'][:, :], p1, p1[:, :], AF.Exp)
                    kb.act(t['DTm'], t['DTm'][:, :], p2, p2[:, :], AF.Exp)
                    kb.op('dve', lambda e: e.tensor_tensor(out=t['Dm'][:, :], in0=t['Dm'][:, :], in1=MS[:, :], op=ALU.mult), reads=[t['Dm'], MS], writes=[t['Dm']])
                    kb.op('dve', lambda e: e.tensor_tensor(out=t['DTm'][:, :], in0=t['DTm'][:, :], in1=BA[:, :], op=ALU.mult), reads=[t['DTm'], BA], writes=[t['DTm']])
                    kb.op('dve', lambda e: e.scalar_tensor_tensor(out=t['Pt'][:, :], in0=p3[:, :], scalar=s_[:, 0, hd:hd + 1], in1=t['Dm'][:, :],
                                                                  op0=ALU.mult, op1=ALU.mult), reads=[p3, s_, t['Dm']], writes=[t['Pt']])
                    p4 = pp.next()
                    kb.op('pe', lambda e: e.transpose(p4[:, :], t['Pt'][:, :], ident_f[:, :]), reads=[t['Pt'], ident_f], writes=[p4])
                    kb.act(t['P'], t['P'][:, :], p4, p4[:, :], AF.Copy)
                if O2B_LVL < 3:
                    for hd in range(4):
                        kb.op('dve', lambda e: e.tensor_copy(out=o_[:, hd, :], in_=T_[hd]['R'][:, :]), reads=[T_[hd]['R']], writes=[o_])
                    kb.dma('pool', S['OF'][a0:a0 + 128, :], o_[:, :, :].rearrange("p h f -> p (h f)"), reads=[o_])
                    continue
                for hd in range(4):
                    t = T_[hd]
                    for nm, src, ki in (("M0", 'P', 0), ("M0t", 'Pt', 0), ("C1", 'P', 1), ("C1t", 'Pt', 1), ("C2", 'P', 2), ("C2t", 'Pt', 2),
                                        ("C3", 'P', 3), ("C3t", 'Pt', 3), ("C4", 'P', 4), ("C4t", 'Pt', 4)):
                        if nm == "C4":
                            continue
                        kb.op('dve', lambda e: e.tensor_tensor(out=t[nm][:, :], in0=t[src][:, :], in1=KM[:, ki, :], op=ALU.mult),
                              reads=[t[src], KM], writes=[t[nm]])
                    kb.op('dve', lambda e: e.tensor_tensor(out=t['R'][:, :], in0=t['M0'][:, :], in1=ident_f[:, :], op=ALU.add), reads=[t['M0'], ident_f], writes=[t['R']])
                    kb.op('dve', lambda e: e.tensor_tensor(out=t['Rt'][:, :], in0=t['M0t'][:, :], in1=ident_f[:, :], op=ALU.add), reads=[t['M0t'], ident_f], writes=[t['Rt']])
                for st in range(2):
                    for hd in range(4):
                        t = T_[hd]
                        pa, pb_ = pp.next(), pp.next()
                        kb.mm(pa, pa[:, :], t['M0t'], t['M0t'][:, :], t['M0'], t['M0'][:, :])
                        kb.mm(pb_, pb_[:, :], t['M0'], t['M0'][:, :], t['M0t'], t['M0t'][:, :])
                        kb.act(t['X'], t['X'][:, :], pa, pa[:, :], AF.Copy)
                        kb.act(t['Xt'], t['Xt'][:, :], pb_, pb_[:, :], AF.Copy)
                        pc, pd = pp.next(), pp.next()
                        kb.mm(pc, pc[:, :], t['Xt'], t['Xt'][:, :], t['R'], t['R'][:, :])
                        kb.mm(pd, pd[:, :], t['X'], t['X'][:, :], t['Rt'], t['Rt'][:, :])
                        kb.op('dve', lambda e: e.tensor_tensor(out=t['R'][:, :], in0=t['R'][:, :], in1=pc[:, :], op=ALU.add), reads=[t['R'], pc], writes=[t['R']])
                        kb.op('dve', lambda e: e.tensor_tensor(out=t['Rt'][:, :], in0=t['Rt'][:, :], in1=pd[:, :], op=ALU.add), reads=[t['Rt'], pd], writes=[t['Rt']])
                        t['M0'], t['X'] = t['X'], t['M0']
                        t['M0t'], t['Xt'] = t['Xt'], t['M0t']
                for lv in range(1, 5):
                    for hd in range(4):
                        t = T_[hd]
                        cm, cmt = t.get('C%d' % lv), t['C%dt' % lv]
                        pa = pp.next()
                        kb.mm(pa, pa[:, :], cmt, cmt[:, :], t['R'], t['R'][:, :])
                        kb.act(t['X'], t['X'][:, :], pa, pa[:, :], AF.Copy)
                        if lv < 4:
                            pb_ = pp.next()
                            kb.mm(pb_, pb_[:, :], cm, cm[:, :], t['Rt'], t['Rt'][:, :])
                            kb.act(t['Xt'], t['Xt'][:, :], pb_, pb_[:, :], AF.Copy)
                        pc = pp.next()
                        kb.mm(pc, pc[:, :], t['Rt'], t['Rt'][:, :], t['X'], t['X'][:, :])
                        if lv < 4:
                            pd = pp.next()
                            kb.mm(pd, pd[:, :], t['R'], t['R'][:, :], t['Xt'], t['Xt'][:, :])
                        kb.op('dve', lambda e: e.tensor_tensor(out=t['R'][:, :], in0=t['R'][:, :], in1=pc[:, :], op=ALU.add), reads=[t['R'], pc], writes=[t['R']])
                        if lv < 4:
                            kb.op('dve', lambda e: e.tensor_tensor(out=t['Rt'][:, :], in0=t['Rt'][:, :], in1=pd[:, :], op=ALU.add), reads=[t['Rt'], pd], writes=[t['Rt']])
                if O2B_LVL < 4:
                    for hd in range(4):
                        kb.op('dve', lambda e: e.tensor_copy(out=o_[:, hd, :], in_=T_[hd]['R'][:, :]), reads=[T_[hd]['R']], writes=[o_])
                    kb.dma('pool', S['OF'][a0:a0 + 128, :], o_[:, :, :].rearrange("p h f -> p (h f)"), reads=[o_])
                    continue
                for hd in range(4):
                    t = T_[hd]
                    qT, kT = q_[:, hd, :], q_[:, 4 + hd, :]
                    Kt, Vt = k_[:, hd, 0:128], k_[:, hd, 128:256]
                    kb.op('dve', lambda e: e.tensor_scalar(out=t['ru'][:, :], in0=Vt, scalar1=b_[:, d * 4 + hd:d * 4 + hd + 1], scalar2=None, op0=ALU.mult),
                          reads=[k_, b_], writes=[t['ru']])
                    kb.op('dve', lambda e: e.tensor_scalar(out=t['rw'][:, :], in0=Kt, scalar1=s_[:, 4, hd:hd + 1], scalar2=None, op0=ALU.mult),
                          reads=[k_, s_], writes=[t['rw']])
                    kb.op('dve', lambda e: e.tensor_scalar(out=t['Kp'][:, :], in0=Kt, scalar1=s_[:, 2, hd:hd + 1], scalar2=None, op0=ALU.mult),
                          reads=[k_, s_], writes=[t['Kp']])
                    p1, p2, p3 = pp.next(), pp.next(), pp.next()
                    kb.mm(p1, p1[:, :], t['R'], t['R'][:, :], t['ru'], t['ru'][:, :])
                    kb.mm(p2, p2[:, :], t['rw'], t['rw'][:, :], t['R'], t['R'][:, :])
                    kb.mm(p3, p3[:, :], q_, kT, q_, qT)
                    kb.act(t['u'], t['u'][:, :], p1, p1[:, :], AF.Copy)
                    kb.act(t['wT'], t['wT'][:, :], p2, p2[:, :], AF.Copy)
                    kb.op('dve', lambda e: e.tensor_tensor(out=t['AT'][:, :], in0=p3[:, :], in1=t['DTm'][:, :], op=ALU.mult), reads=[p3, t['DTm']], writes=[t['AT']])
                    p5, p6, p7, p8 = pp.next(), pp.next(), pp.next(), pp.next()
                    kb.mm(p5, p5[:, :], t['wT'], t['wT'][:, :], St[hd], St[hd][:, :])
                    kb.mm(p6, p6[:, :], q_, qT, St[hd], St[hd][:, :])
                    kb.op('dve', lambda e: e.tensor_tensor(out=t['vn'][:, :], in0=t['u'][:, :], in1=p5[:, :], op=ALU.subtract), reads=[t['u'], p5], writes=[t['vn']])
                    kb.mm(p7, p7[:, :], t['AT'], t['AT'][:, :], t['vn'], t['vn'][:, :])
                    kb.mm(p8, p8[:, :], t['Kp'], t['Kp'][:, :], t['vn'], t['vn'][:, :])
                    kb.act(t['o2'], t['o2'][:, :], p7, p7[:, :], AF.Copy)
                    kb.op('dve', lambda e: e.scalar_tensor_tensor(out=o_[:, hd, :], in0=p6[:, :], scalar=s_[:, 1, hd:hd + 1], in1=t['o2'][:, :],
                                                                  op0=ALU.mult, op1=ALU.add), reads=[p6, s_, t['o2']], writes=[o_])
                    kb.op('dve', lambda e: e.scalar_tensor_tensor(out=St[hd][:, :], in0=St[hd][:, :], scalar=s_[:, 3, hd:hd + 1], in1=p8[:, :],
                                                                  op0=ALU.mult, op1=ALU.add), reads=[St[hd], s_, p8], writes=[St[hd]])
                if not rev:
                    kb.dma('pool', S['OF'][a0:a0 + 128, :], o_[:, :, :].rearrange("p h f -> p (h f)"), reads=[o_])
                else:
                    f_ = oft.next()
                    kb.dma('sp', f_[:, :, :], S['OF'][a0:a0 + 128, :].rearrange("p (h f) -> p h f", f=128), writes=[f_])
                    kb.op('dve', lambda e: e.tensor_tensor(out=o_[:, :, :], in0=o_[:, :, :], in1=f_[:, :, :], op=ALU.add), reads=[o_, f_], writes=[o_])
                    kb.dma('pool', S['OS'][a0:a0 + 128, :], o_[:, :, :].rearrange("p h f -> p (h f)"), reads=[o_])
            t0 += Sq
        kb.barrier()
    kb.ph = None


DGROUPS = ((128, 1), (512, 4), (2048, 16))
EV_PAD = 640
EV_L = 2 * 1024 + 1 + 2 * EV_PAD


def t5_bucket_np(rel):
    nb, max_exact = 16, 8
    n = np.abs(rel)
    large = max_exact + (np.log(np.maximum(n, 1) / max_exact) / math.log(1024 / max_exact) * (nb - max_exact)).astype(np.int64)
    large = np.minimum(large, nb - 1)
    return (np.where(rel > 0, nb, 0) + np.where(n < max_exact, n, large)).astype(np.int32)


def phase_o3(kb, P, C, S, seqs):
    with ExitStack() as ph:
        kb.ph = ph
        rb = kb.sb([32, 12], F32, "rb")
        kb.dma('sp', rb[:, :], P['rel_bias'], writes=[rb])
        oh = kb.sb([32, 3, 129], F32, "oh")
        kb.dma('sp', oh[:, :, :], C['onehot'], writes=[oh])
        ev = kb.sb([4, 3, EV_L], F32, "ev")
        kb.op('dve', lambda e: e.memset(ev[:, :, :], 0.0), writes=[ev])
        pbv = kb.ps([4, 129], F32, "pbv")
        for gi, (win, dil) in enumerate(DGROUPS):
            kb.mm(pbv, pbv[:, :], rb, rb[:, gi * 4:(gi + 1) * 4], oh, oh[:, gi, :])
            c0 = EV_PAD + 1024 - 64 * dil
            kb.act(ev, ev[:, gi, c0:c0 + 128 * dil + 1:dil], pbv, pbv[:, :], AF.Exp)
        kb.dma('pool', S['EV'], ev[:, :, :], reads=[ev])
        kb.barrier()
        rng_ = []
        for gi, (win, dil) in enumerate(DGROUPS):
            w = 64 * dil
            lo = -((w + 127) // 128)
            hi = 4 + (w + 127) // 128
            rng_.append((lo, hi))
        ntile = sum(hi - lo for lo, hi in rng_)
        Et = kb.sb([128, ntile, 512], F32, "Et")
        hk = Rot([kb.sb([128, 512], F32, "hk") for _ in range(2)])
        kt = Rot([kb.sb([128, 3, 2560], BF16, "kt") for _ in range(2)])
        vt = Rot([kb.sb([128, 3, 20, 65], BF16, "vt") for _ in range(2)])
        qt = Rot([kb.sb([128, 3, 512], BF16, "qt") for _ in range(2)])
        es = Rot([kb.sb([128, 512], F32, "es") for _ in range(3)])
        pe_ = Rot([kb.sb([128, 512], BF16, "pexp") for _ in range(3)])
        ones = kb.sb([128, 64], F32, "ones")
        kb.op('dve', lambda e: e.memset(ones[:, :], 1.0), writes=[ones])
        num = Rot([kb.sb([65, 512], F32, "num") for _ in range(2)])
        rec = Rot([kb.sb([64, 512], F32, "rec") for _ in range(2)])
        yt = Rot([kb.sb([64, 512], BF16, "yt") for _ in range(2)])
        pss = Rot([kb.ps([128, 512], F32, "pss") for _ in range(4)])
        pac = Rot([kb.ps([65, 512], F32, "pac") for _ in range(2)])
        pdn = Rot([kb.ps([64, 512], F32, "pdn") for _ in range(1)])
        for slot in range(4):
            ti = 0
            for gi, (win, dil) in enumerate(DGROUPS):
                lo, hi = rng_[gi]
                for r in range(lo, hi):
                    h_ = hk.next()
                    base = r * 128 + EV_PAD + 1024 - 511
                    src = bass.AP(S['EV'].tensor, (slot * 3 + gi) * EV_L + base, [[1, 128], [1, 512]])
                    kb.dma('sp', h_[:, :], src, writes=[h_])
                    kb.op('dve', lambda e: e.tensor_copy(out=Et[:, ti, :], in_=h_[:, ::-1]), reads=[h_], writes=[Et])
                    ti += 1
            t0 = 0
            for Sq in seqs:
                for qi in range(Sq // 512):
                    q0 = qi * 512
                    q_, k_, v_ = qt.next(), kt.next(), vt.next()
                    spans = []
                    for gi in range(3):
                        hh = gi * 4 + slot
                        ch, pb = hh // 2, (hh % 2) * 64
                        lo, hi = rng_[gi]
                        klo, khi = max(q0 + lo * 128, 0), min(q0 + hi * 128, Sq)
                        spans.append((klo, khi, pb))
                        nk = khi - klo
                        kb.dma('sp', q_[pb:pb + 64, gi, :], S['DQK'][hh * 64:(hh + 1) * 64, t0 + q0:t0 + q0 + 512], writes=[q_])
                        kb.dma('sp', k_[pb:pb + 64, gi, 0:nk], S['DQK'][768 + hh * 64:768 + (hh + 1) * 64, t0 + klo:t0 + khi], writes=[k_])
                        for v0 in range(0, nk // 128, 8):
                            v1 = min(v0 + 8, nk // 128)
                            kb.dma('sp', v_[:, gi, v0:v1, :],
                                   S['DV'][t0 + klo + v0 * 128:t0 + klo + v1 * 128, hh * 65:(hh + 1) * 65].rearrange("(k p) f -> p k f", p=128),
                                   writes=[v_])
                    acc = pac.next()
                    total = sum((khi - klo) // 128 for klo, khi, _ in spans)
                    cnt = 0
                    tb = 0
                    for gi in range(3):
                        lo, hi = rng_[gi]
                        klo, khi, pb = spans[gi]
                        for kk in range((khi - klo) // 128):
                            r = (klo + kk * 128 - q0) // 128
                            e_idx = tb + (r - lo)
                            p_s, e_s, p_e = pss.next(), es.next(), pe_.next()
                            kb.mm(p_s, p_s[:, :], k_, k_[pb:pb + 64, gi, kk * 128:(kk + 1) * 128], q_, q_[pb:pb + 64, gi, :])
                            kb.act(e_s, e_s[:, :], p_s, p_s[:, :], AF.Exp)
                            kb.op('dve', lambda e: e.tensor_tensor(out=p_e[:, :], in0=e_s[:, :], in1=Et[:, e_idx, :], op=ALU.mult),
                                  reads=[e_s, Et], writes=[p_e])
                            kb.mm(acc, acc[:, :], v_, v_[:, gi, kk, :], p_e, p_e[:, :], start=(cnt == 0), stop=(cnt == total - 1))
                            cnt += 1
                        tb += hi - lo
                    n_, r_, y_, p_d = num.next(), rec.next(), yt.next(), pdn.next()
                    kb.act(n_, n_[:, :], acc, acc[:, :], AF.Copy)
                    kb.mm(p_d, p_d[:, :], ones, ones[64:65, 0:64], n_, n_[64:65, :])
                    kb.op('dve', lambda e: e.reciprocal(out=r_[:, :], in_=p_d[:, :]), reads=[p_d], writes=[r_])
                    kb.op('dve', lambda e: e.tensor_tensor(out=y_[:, :], in0=n_[0:64, :], in1=r_[:, :], op=ALU.mult), reads=[n_, r_], writes=[y_])
                    kb.dma('pool', S['YD'][slot * 64:(slot + 1) * 64, t0 + q0:t0 + q0 + 512], y_[:, :], reads=[y_])
                t0 += Sq
        kb.barrier()
    kb.ph = None


def phase_o4(kb, P, C, S, j, T):
    with ExitStack() as ph:
        kb.ph = ph
        ident_bf, ident_f, epsb = make_consts(kb, C)
        gn = load_bcast(kb, P['c_norm'][j], 128, "gn")
        ost = Rot([kb.sb([128, 4, 128], F32, "os") for _ in range(2)])
        zst = Rot([kb.sb([128, 4, 128], F32, "zs") for _ in range(2)])
        sq = Rot([kb.sb([128, 4, 128], F32, "sq") for _ in range(2)])
        ss = Rot([kb.sb([128, 4], F32, "ss") for _ in range(2)])
        rs = Rot([kb.sb([128, 4], F32, "rs") for _ in range(2)])
        yb = Rot([kb.sb([128, 4, 128], BF16, "yb") for _ in range(2)])
        yT = Rot([kb.sb([128, 4, 128], BF16, "yT") for _ in range(2)])
        pt = Rot([kb.ps([128, 4, 128], BF16, "pt") for _ in range(2)])
        for it in range(T // 128):
            r0 = it * 128
            o_, z_, s_, s1, r1, y_, yT_, p_ = ost.next(), zst.next(), sq.next(), ss.next(), rs.next(), yb.next(), yT.next(), pt.next()
            kb.dma('sp', o_[:, :, :], S['OS'][r0:r0 + 128, :].rearrange("p (h f) -> p h f", f=128), writes=[o_])
            kb.dma('sp', z_[:, :, :], S['ZS'][r0:r0 + 128, :].rearrange("p (h f) -> p h f", f=128), writes=[z_])
            kb.act(s_, s_[:, :, :], o_, o_[:, :, :], AF.Square)
            kb.op('dve', lambda e: e.tensor_reduce(out=s1[:, :], in_=s_[:, :, :], axis=AX.X, op=ALU.add), reads=[s_], writes=[s1])
            kb.act(s1, s1[:, :], s1, s1[:, :], AF.Sqrt, scale=1.0 / 128, bias=epsb[:, :], extra_reads=[epsb])
            kb.op('dve', lambda e: e.reciprocal(out=r1[:, :], in_=s1[:, :]), reads=[s1], writes=[r1])
            kb.op('dve', lambda e: e.tensor_tensor(out=o_[:, :, :], in0=o_[:, :, :], in1=r1[:, :].unsqueeze(2).to_broadcast([128, 4, 128]), op=ALU.mult),
                  reads=[o_, r1], writes=[o_])
            kb.op('dve', lambda e: e.tensor_tensor(out=o_[:, :, :], in0=o_[:, :, :], in1=gn[:, :].unsqueeze(1).to_broadcast([128, 4, 128]), op=ALU.mult),
                  reads=[o_, gn], writes=[o_])
            kb.op('dve', lambda e: e.tensor_tensor(out=y_[:, :, :], in0=o_[:, :, :], in1=z_[:, :, :], op=ALU.mult), reads=[o_, z_], writes=[y_])
            for c in range(4):
                kb.op('pe', lambda e: e.transpose(p_[:, c, :], y_[:, c, :], ident_bf[:, :]), reads=[y_, ident_bf], writes=[p_])
            kb.act(yT_, yT_[:, :, :], p_, p_[:, :, :], AF.Copy)
            kb.dma('pool', S['YC'][:, r0:r0 + 128].rearrange("(c p) t -> p c t", p=128), yT_[:, :, :], reads=[yT_])
        kb.barrier()
    kb.ph = None


def build(seqs, plan):
    T = sum(seqs)
    kb = KB()
    nc = kb.nc
    P = {}
    shapes = dict(rel_bias=(32, 12), norm_mix=(4, D), norm_ff=(4, D), norm_final=(D,), w_ff1=(4, D, 4 * D),
                  w_ff2=(4, 4 * D, D), w_in_e=(2, D, 1792), w_out_e=(2, D, D), a_qnorm=(2, 64), a_knorm=(2, 64),
                  b_conv_w=(2, 4, 512), b_conv_b=(2, 512), b_wr=(2, 2, 8, 64, 64), b_br=(2, 2, 512),
                  b_wi=(2, 2, 8, 64, 64), b_bi=(2, 2, 512), b_lambda=(2, 2, 512), w_in_o=(2, D, 4368),
                  w_out_o=(2, 768, D), c_conv_w=(2, 4, 1536), c_a_log=(2, 2, 4), c_dt_bias=(2, 2, 4), c_norm=(2, 128))
    for k, sh in shapes.items():
        P[k] = kb.dram(k, sh, F32, kind="ExternalInput")
    x_in = kb.dram("x", (T, D), F32, kind="ExternalInput")
    C = {'ident': kb.dram("ident", (128, 128), F32, kind="ExternalInput"),
         'rope': kb.dram("rope", (T, 64), F32, kind="ExternalInput")}
    S = {'QT': kb.dram("s_qt", (512, T), BF16), 'KTD': kb.dram("s_ktd", (256, T), BF16), 'VA': kb.dram("s_va", (T, 130), BF16),
         'XR': kb.dram("s_xr", (512, T), F32), 'GG': kb.dram("s_gg", (512, T), F32), 'HF': kb.dram("s_hf", (512, T), F32),
         'YB': kb.dram("s_yb", (512, T), BF16), 'YA': kb.dram("s_ya", (512, T), BF16),
         'QKV': kb.dram("s_qkv", (1536, T), F32), 'ZS': kb.dram("s_zs", (T, 512), F32), 'BG': kb.dram("s_bg", (T, 16), F32),
         'DQK': kb.dram("s_dqk", (1536, T), BF16), 'DV': kb.dram("s_dv", (T, 780), BF16),
         'CQK': kb.dram("s_cqk", (1024, T), F32), 'CKV': kb.dram("s_ckv", (T, 1024), F32),
         'OF': kb.dram("s_of", (T, 512), F32), 'OS': kb.dram("s_os", (T, 512), F32),
         'EV': kb.dram("s_ev", (4, 3, EV_L), F32), 'YD': kb.dram("s_yd", (256, T), BF16), 'YC': kb.dram("s_yc", (512, T), BF16)}
    C['ba'] = kb.dram("c_ba", (2, 128, 128), F32, kind="ExternalInput")
    C['ms'] = kb.dram("c_ms", (2, 128, 128), F32, kind="ExternalInput")
    C['onehot'] = kb.dram("c_onehot", (32, 3, 129), F32, kind="ExternalInput")
    C['km'] = kb.dram("c_km", (5, 128, 128), F32, kind="ExternalInput")
    y = kb.dram("y", (T, D), F32, kind="ExternalOutput")
    xs = kb.dram("xs", (T, D), F32)
    cur = x_in
    for i, (name, layer) in enumerate(plan):
        last = (i == len(plan) - 1)
        if name == 'M':
            phase_mlp(kb, P, C, cur, xs, layer, T, final_out=(y if last else None))
            cur = xs
        elif name == 'E':
            j = layer // 2
            phase_e1(kb, P, C, S, cur, j, layer, T)
            phase_e2(kb, P, C, S, j, seqs, False)
            phase_e2(kb, P, C, S, j, seqs, True)
            phase_e3(kb, P, C, S, seqs)
            phase_outproj(kb, P, C, S, P['w_out_e'][j], [(S['YA'], 4), (S['YB'], 4)], cur, (y if last else xs), T)
            cur = xs
        elif name == 'O':
            j = layer // 2
            phase_o1(kb, P, C, S, cur, j, layer, T)
            if O_STOP >= 2:
                phase_o2a(kb, P, C, S, j, seqs)
            if O_STOP >= 3:
                phase_o2b(kb, P, C, S, seqs, False)
            if O_STOP >= 4:
                phase_o2b(kb, P, C, S, seqs, True)
            if O_STOP >= 5:
                phase_o3(kb, P, C, S, seqs)
            if O_STOP >= 6:
                phase_o4(kb, P, C, S, j, T)
            if O_STOP >= 7:
                phase_outproj(kb, P, C, S, P['w_out_o'][j], [(S['YC'], 4), (S['YD'], 2)], cur, (y if last else xs), T)
            cur = xs
    kb.barrier()
    kb.stack.close()
    return kb


SEQS = [16384, 2048, 2048]
PLAN = [('E', 0), ('M', 0), ('O', 1), ('M', 1), ('E', 2), ('M', 2), ('O', 3), ('M', 3)]


def host_consts(seqs=None):
    seqs = SEQS if seqs is None else seqs
    tabs = []
    for Sq in seqs:
        pos = np.arange(Sq)
        row = (pos // 64).astype(np.float32)
        col = (pos % 64).astype(np.float32)
        inv = (np.float32(10000.0) ** (-np.arange(16, dtype=np.float32) / np.float32(16))).astype(np.float32)
        ang = np.concatenate([row[:, None] * inv[None], col[:, None] * inv[None]], axis=1).astype(np.float32)
        tabs.append(np.concatenate([np.cos(ang), np.sin(ang)], axis=1).astype(np.float32))
    k_ = np.arange(128)[:, None]
    i_ = np.arange(128)[None, :]
    ba = np.stack([(k_ <= i_), (k_ >= i_)]).astype(np.float32)
    ms = np.stack([(i_ < k_), (i_ > k_)]).astype(np.float32)
    onehot = np.zeros((32, 3, 129), np.float32)
    for gi, (win, dil) in enumerate(DGROUPS):
        b = t5_bucket_np(np.arange(-64, 65) * dil)
        onehot[b, gi, np.arange(129)] = 1.0
    km = [(k_ // 8 == i_ // 8)]
    for sz in (8, 16, 32, 64):
        km.append((k_ // (2 * sz) == i_ // (2 * sz)) & (k_ // sz != i_ // sz))
    km = np.stack(km).astype(np.float32)
    return {'ident': np.eye(128, dtype=np.float32), 'rope': np.ascontiguousarray(np.concatenate(tabs, 0)),
            'c_ba': ba, 'c_ms': ms, 'c_onehot': onehot, 'c_km': km}


def kernel(**inputs):
    xp = np.asarray(inputs['x_prompt'])
    xsm = np.asarray(inputs['x_sample'])
    kb = build(SEQS, PLAN)
    params = {k: np.ascontiguousarray(np.asarray(v), dtype=np.float32) for k, v in inputs.items()
              if k not in ('x_prompt', 'x_sample')}
    hc = host_consts()
    in_maps = []
    for c in range(8):
        x = np.concatenate([xp[c % 2], xsm[2 * c], xsm[2 * c + 1]], axis=0)
        m = dict(params)
        m.update(hc)
        m['x'] = np.ascontiguousarray(x)
        in_maps.append(m)
    res = run_bass_kernel_spmd(kb.nc, in_maps, core_ids=list(range(8)))
    yp = np.stack([res.results[b]['y'][:16384] for b in range(2)], axis=0)
    ys = np.stack([res.results[c // 2]['y'][16384 + (c % 2) * 2048:16384 + (c % 2 + 1) * 2048] for c in range(16)], axis=0)
    return (yp.astype(np.float32), ys.astype(np.float32))
```

```python
import math
from contextlib import ExitStack
import numpy as np
import concourse.bass as bass
import concourse.mybir as mybir
from concourse.bass_utils import run_bass_kernel_spmd

F32, BF16 = mybir.dt.float32, mybir.dt.bfloat16
ALU, AF, AX = mybir.AluOpType, mybir.ActivationFunctionType, mybir.AxisListType
D = 1024
EPS = 1e-6
NDS = 24
DEBUG = False
O_STOP = 99
O2B_LVL = 9


class Tile:
    __slots__ = ("h", "w", "r", "psum")

    def __init__(s, h, psum=False):
        s.h = h
        s.w = None
        s.r = {}
        s.psum = psum

    def __getitem__(s, k):
        return s.h[k]


class View:
    __slots__ = ("h", "par", "psum")

    def __init__(s, par, ap):
        s.h = ap
        s.par = par
        s.psum = par.psum

    def __getitem__(s, k):
        return s.h[k]

    w = property(lambda s: s.par.w, lambda s, v: setattr(s.par, 'w', v))
    r = property(lambda s: s.par.r, lambda s, v: setattr(s.par, 'r', v))


class KB:
    def __init__(s):
        s.nc = bass.Bass("TRN2", target_bir_lowering=False)
        nc = s.nc
        s.eng = {'pe': nc.tensor, 'act': nc.scalar, 'dve': nc.vector, 'pool': nc.gpsimd, 'sp': nc.sync}
        s.stack = ExitStack()
        s.sem = {e: s.stack.enter_context(nc.semaphore("c_" + e)) for e in ('pe', 'act', 'dve', 'pool')}
        s.cnt = {e: 0 for e in s.sem}
        s.waited = {e: {} for e in s.eng}
        s.dsem = [[s.stack.enter_context(nc.semaphore("d%d" % i)), 0] for i in range(NDS)]
        s.dnext = 0
        s.uid = 0
        s.ph = None

    def sb(s, shape, dt=F32, name="t"):
        s.uid += 1
        return Tile(s.ph.enter_context(s.nc.sbuf_tensor("%s_%d" % (name, s.uid), list(shape), dt)))

    def ps(s, shape, dt=F32, name="p"):
        s.uid += 1
        bank = 512 if dt == F32 else 1024
        h = s.ph.enter_context(s.nc.psum_tensor("%s_%d" % (name, s.uid), [128, bank], dt))
        n = 1
        for v in shape[1:]:
            n *= v
        assert n <= bank
        ap = h[0:shape[0], 0:n]
        if len(shape) == 3:
            ap = ap.rearrange("p (a b) -> p a b", b=shape[2])
        elif len(shape) == 4:
            ap = ap.rearrange("p (a b c) -> p a b c", b=shape[2], c=shape[3])
        return Tile(ap, psum=True)

    def dram(s, name, shape, dt=F32, kind="Internal"):
        if kind == "Internal" and DEBUG:
            kind = "ExternalOutput"
        return s.nc.dram_tensor(name, list(shape), dt, kind=kind).ap()

    def _wait(s, e, deps):
        eng = s.eng[e]
        wd = s.waited[e]
        for sem, val in deps.items():
            if wd.get(sem, 0) < val:
                eng.wait_ge(sem, val)
                wd[sem] = val

    @staticmethod
    def _deps(reads, writes):
        d = {}
        for t in reads:
            if t.w is not None and d.get(t.w[0], 0) < t.w[1]:
                d[t.w[0]] = t.w[1]
        for t in writes:
            if t.w is not None and d.get(t.w[0], 0) < t.w[1]:
                d[t.w[0]] = t.w[1]
            for sem, val in t.r.items():
                if d.get(sem, 0) < val:
                    d[sem] = val
        return d

    @staticmethod
    def _mark(tok, reads, writes):
        for t in reads:
            if t.r.get(tok[0], 0) < tok[1]:
                t.r[tok[0]] = tok[1]
        for t in writes:
            t.w = tok
            t.r = {}

    def op(s, e, fn, reads=(), writes=()):
        px = [t for t in reads if t.psum]
        if px:
            reads = [t for t in reads if not t.psum]
            writes = list(writes) + px
        d = s._deps(reads, writes)
        if e == 'pe':
            d.pop(s.sem['pe'], None)
        else:
            own = s.sem[e]
            if own in d:
                raw = 0
                for t in list(reads) + px:
                    if t.w is not None and t.w[0] == own and t.w[1] > raw:
                        raw = t.w[1]
                if raw:
                    d[own] = raw
                else:
                    del d[own]
        s._wait(e, d)
        ins = fn(s.eng[e])
        s.cnt[e] += 1
        ins.then_inc(s.sem[e], 1)
        s._mark((s.sem[e], s.cnt[e]), reads, writes)

    def dma(s, e, out, in_, reads=(), writes=(), slow=False):
        d = s._deps(reads, writes)
        slot = s.dsem[s.dnext]
        s.dnext = (s.dnext + 1) % len(s.dsem)
        sem, c = slot
        if c > 0:
            d[sem] = max(d.get(sem, 0), c)
        s._wait(e, d)
        if slow:
            s.eng[e].dma_start(out=out, in_=in_, allow_slow_non_contiguous=True).then_inc(sem, 16)
        else:
            s.eng[e].dma_start(out=out, in_=in_).then_inc(sem, 16)
        slot[1] = c + 16
        s._mark((sem, c + 16), reads, writes)

    def barrier(s):
        d = {s.sem[f]: s.cnt[f] for f in s.sem if s.cnt[f] > 0}
        for sem, c in s.dsem:
            if c > 0:
                d[sem] = c
        for e in s.eng:
            s._wait(e, dict(d))

    def mm(s, out_t, out_ap, lhsT_t, lhsT_ap, rhs_t, rhs_ap, start=True, stop=True):
        s.op('pe', lambda e: e.matmul(out_ap, lhsT=lhsT_ap, rhs=rhs_ap, start=start, stop=stop),
             reads=[lhsT_t, rhs_t] + ([] if start else [out_t]), writes=[out_t])

    def act(s, out_t, out_ap, in_t, in_ap, func, extra_reads=(), **kw):
        s.op('act', lambda e: e.activation(out=out_ap, in_=in_ap, func=func, **kw),
             reads=[in_t] + list(extra_reads), writes=[out_t])


class Rot:
    def __init__(s, tiles):
        s.t = tiles
        s.i = 0

    def next(s):
        t = s.t[s.i]
        s.i = (s.i + 1) % len(s.t)
        return t


def load_w_bf16(kb, w_ap, K, N, name):
    nch = K // 128
    t = kb.sb([128, nch, N], BF16, name)
    for c in range(nch):
        kb.dma('pool', t[:, c, :], w_ap[c * 128:(c + 1) * 128, :], writes=[t])
    return t


def load_col(kb, vec_ap, n, name):
    t = kb.sb([128, n], F32, name)
    kb.dma('sp', t[:, :], vec_ap.rearrange("(c p) -> p c", p=128), writes=[t], slow=True)
    return t


def load_bcast(kb, vec_ap, n, name):
    t = kb.sb([128, n], F32, name)
    kb.dma('sp', t[:, :], vec_ap.partition_broadcast(128), writes=[t])
    return t


class NormT:
    def __init__(s, kb, g_col, ident_bf):
        s.kb = kb
        s.g = g_col
        s.ident = ident_bf
        s.sq = Rot([kb.sb([128, D], BF16, "nsq") for _ in range(2)])
        s.ss = Rot([kb.sb([128, 1], F32, "nss") for _ in range(2)])
        s.rs = Rot([kb.sb([128, 1], F32, "nrs") for _ in range(2)])
        s.hb = Rot([kb.sb([128, D], BF16, "nhb") for _ in range(2)])
        s.pt = Rot([kb.ps([128, 8, 128], BF16, "npt") for _ in range(2)])

    def rstd(s, x_t, x_ap):
        kb = s.kb
        sq, ss, rs = s.sq.next(), s.ss.next(), s.rs.next()
        kb.op('act', lambda e: e.activation(out=sq[:, :], in_=x_ap, func=AF.Square, accum_out=ss[:, :]),
              reads=[x_t], writes=[sq, ss])
        kb.act(ss, ss[:, :], ss, ss[:, :], AF.Sqrt, scale=1.0 / D, bias=s.epsb[:, :], extra_reads=[s.epsb])
        kb.op('dve', lambda e: e.reciprocal(out=rs[:, :], in_=ss[:, :]), reads=[ss], writes=[rs])
        return rs

    def __call__(s, x_t, x_ap, out_t, out_ap):
        kb = s.kb
        rs = s.rstd(x_t, x_ap)
        hb, pt = s.hb.next(), s.pt.next()
        kb.op('dve', lambda e: e.tensor_scalar(out=hb[:, :], in0=x_ap, scalar1=rs[:, :], scalar2=None, op0=ALU.mult),
              reads=[x_t, rs], writes=[hb])
        for c in range(8):
            kb.op('pe', lambda e: e.transpose(pt[:, c, :], hb[:, c * 128:(c + 1) * 128], s.ident[:, :]),
                  reads=[hb, s.ident], writes=[pt])
        kb.op('dve', lambda e: e.tensor_tensor(out=out_ap, in0=pt[:, :, :],
                                               in1=s.g[:, :].unsqueeze(2).to_broadcast([128, 8, 128]), op=ALU.mult),
              reads=[pt, s.g], writes=[out_t])


def make_consts(kb, C):
    ident_bf = kb.sb([128, 128], BF16, "identb")
    kb.dma('pool', ident_bf[:, :], C['ident'], writes=[ident_bf])
    ident_f = kb.sb([128, 128], F32, "identf")
    kb.dma('sp', ident_f[:, :], C['ident'], writes=[ident_f])
    epsb = kb.sb([128, 1], F32, "epsb")
    kb.op('dve', lambda e: e.memset(epsb[:, :], EPS), writes=[epsb])
    return ident_bf, ident_f, epsb


def phase_mlp(kb, P, C, x_src, x_dst, layer, T, final_out=None):
    with ExitStack() as ph:
        kb.ph = ph
        ident_bf, ident_f, epsb = make_consts(kb, C)
        w1 = load_w_bf16(kb, P['w_ff1'][layer], D, 4 * D, "w1")
        w2 = load_w_bf16(kb, P['w_ff2'][layer], 4 * D, D, "w2")
        g = load_col(kb, P['norm_ff'][layer], 8, "gff")
        nt = NormT(kb, g, ident_bf)
        nt.epsb = epsb
        if final_out is not None:
            gfin = load_bcast(kb, P['norm_final'], D, "gfin")
        TT = 256
        xs = Rot([kb.sb([128, 2, D], F32, "x") for _ in range(2)])
        hT = Rot([kb.sb([128, 8, TT], BF16, "hT") for _ in range(2)])
        aT = kb.sb([128, 32, TT], BF16, "aT")
        rl = Rot([kb.sb([128, TT], F32, "rl") for _ in range(3)])
        ps1 = Rot([kb.ps([128, 512], F32, "ps1") for _ in range(3)])
        ps2 = Rot([kb.ps([128, 512], F32, "ps2") for _ in range(2)])
        for it in range(T // TT):
            x = xs.next()
            h = hT.next()
            r0 = it * TT
            kb.dma('sp', x[:, :, :], x_src[r0:r0 + TT, :].rearrange("(s p) d -> p s d", p=128), writes=[x])
            for sub in range(2):
                nt(x, x[:, sub, :], h, h[:, :, sub * 128:(sub + 1) * 128])
            for f in range(32):
                p1 = ps1.next()
                for c in range(8):
                    kb.mm(p1, p1[:, 0:TT], w1, w1[:, c, f * 128:(f + 1) * 128], h, h[:, c, :], start=(c == 0), stop=(c == 7))
                r = rl.next()
                kb.act(r, r[:, :], p1, p1[:, 0:TT], AF.Relu)
                kb.op('dve', lambda e: e.tensor_tensor(out=aT[:, f, :], in0=r[:, :], in1=r[:, :], op=ALU.mult),
                      reads=[r], writes=[aT])
            for sub in range(2):
                for half in range(2):
                    p2 = ps2.next()
                    for f in range(32):
                        kb.mm(p2, p2[:, :], aT, aT[:, f, sub * 128:(sub + 1) * 128], w2, w2[:, f, half * 512:(half + 1) * 512],
                              start=(f == 0), stop=(f == 31))
                    kb.op('dve', lambda e: e.tensor_tensor(out=x[:, sub, half * 512:(half + 1) * 512],
                                                           in0=x[:, sub, half * 512:(half + 1) * 512], in1=p2[:, :], op=ALU.add),
                          reads=[x, p2], writes=[x])
            if final_out is None:
                kb.dma('pool', x_dst[r0:r0 + TT, :].rearrange("(s p) d -> p s d", p=128), x[:, :, :], reads=[x])
            else:
                for sub in range(2):
                    rs = nt.rstd(x, x[:, sub, :])
                    kb.op('dve', lambda e: e.scalar_tensor_tensor(out=x[:, sub, :], in0=x[:, sub, :], scalar=rs[:, :],
                                                                  in1=gfin[:, :], op0=ALU.mult, op1=ALU.mult),
                          reads=[x, rs, gfin], writes=[x])
                kb.dma('pool', final_out[r0:r0 + TT, :].rearrange("(s p) d -> p s d", p=128), x[:, :, :], reads=[x])
        kb.barrier()
    kb.ph = None


def phase_e1(kb, P, C, S, x_src, j, layer, T):
    with ExitStack() as ph:
        kb.ph = ph
        ident_bf, ident_f, epsb = make_consts(kb, C)
        w = load_w_bf16(kb, P['w_in_e'][j], D, 1792, "wine")
        g = load_col(kb, P['norm_mix'][layer], 8, "gmix")
        gq = load_bcast(kb, P['a_qnorm'][j], 64, "gq")
        gk = load_bcast(kb, P['a_knorm'][j], 64, "gk")
        kb.op('dve', lambda e: e.tensor_scalar(out=gq[:, :], in0=gq[:, :], scalar1=0.125, scalar2=None, op0=ALU.mult),
              reads=[gq], writes=[gq])
        nt = NormT(kb, g, ident_bf)
        nt.epsb = epsb
        xs = Rot([kb.sb([128, 4, D], F32, "x") for _ in range(2)])
        hT = Rot([kb.sb([128, 8, 512], BF16, "hT") for _ in range(2)])
        cs = Rot([kb.sb([128, 4, 64], F32, "cs") for _ in range(2)])
        pq = Rot([kb.ps([128, 512], F32, "pq") for _ in range(2)])
        pkv = Rot([kb.ps([128, 256], F32, "pkv") for _ in range(1)])
        pf = Rot([kb.ps([128, 512], F32, "pf") for _ in range(2)])
        ptr = Rot([kb.ps([128, 6, 128], BF16, "ptr") for _ in range(1)])
        sq = Rot([kb.sb([128, 10, 64], F32, "sq") for _ in range(2)])
        ss = Rot([kb.sb([128, 10], F32, "ss") for _ in range(2)])
        rs = Rot([kb.sb([128, 10], F32, "rs") for _ in range(2)])
        qn = Rot([kb.sb([128, 10, 64], F32, "qn") for _ in range(2)])
        t1 = Rot([kb.sb([128, 10, 2, 16], F32, "t1") for _ in range(2)])
        t2 = Rot([kb.sb([128, 10, 2, 16], F32, "t2") for _ in range(2)])
        qr = Rot([kb.sb([128, 10, 64], F32, "qr") for _ in range(2)])
        qrb = Rot([kb.sb([128, 12, 64], BF16, "qrb") for _ in range(2)])
        qT = Rot([kb.sb([128, 6, 128], BF16, "qT") for _ in range(2)])
        va = Rot([kb.sb([128, 2, 65], BF16, "va") for _ in range(2)])
        for v in va.t:
            kb.op('dve', lambda e: e.memset(v[:, :, :], 1.0), writes=[v])
        xrt = Rot([kb.sb([128, 4, 512], F32, "xrt") for _ in range(2)])
        ggt = Rot([kb.sb([128, 4, 512], F32, "ggt") for _ in range(2)])
        for grp in range(T // 512):
            x, h, c_s = xs.next(), hT.next(), cs.next()
            r0 = grp * 512
            kb.dma('sp', x[:, :, :], x_src[r0:r0 + 512, :].rearrange("(s p) d -> p s d", p=128), writes=[x])
            kb.dma('sp', c_s[:, :, :], C['rope'][r0:r0 + 512, :].rearrange("(s p) d -> p s d", p=128), writes=[c_s])
            for sub in range(4):
                nt(x, x[:, sub, :], h, h[:, :, sub * 128:(sub + 1) * 128])
            for sub in range(4):
                p_q, p_kv = pq.next(), pkv.next()
                hs = slice(sub * 128, (sub + 1) * 128)
                for c in range(8):
                    kb.mm(p_q, p_q[:, :], h, h[:, c, hs], w, w[:, c, 0:512], start=(c == 0), stop=(c == 7))
                for c in range(8):
                    kb.mm(p_kv, p_kv[:, :], h, h[:, c, hs], w, w[:, c, 512:768], start=(c == 0), stop=(c == 7))
                s_q, s_s, r_s, q_n = sq.next(), ss.next(), rs.next(), qn.next()
                kb.act(s_q, s_q[:, 0:8, :], p_q, p_q[:, :].rearrange("p (h d) -> p h d", d=64), AF.Square)
                kb.act(s_q, s_q[:, 8:10, :], p_kv, p_kv[:, 0:128].rearrange("p (h d) -> p h d", d=64), AF.Square)
                kb.op('dve', lambda e: e.tensor_reduce(out=s_s[:, :], in_=s_q[:, :, :], axis=AX.X, op=ALU.add),
                      reads=[s_q], writes=[s_s])
                kb.act(s_s, s_s[:, :], s_s, s_s[:, :], AF.Sqrt, scale=1.0 / 64, bias=epsb[:, :], extra_reads=[epsb])
                kb.op('dve', lambda e: e.reciprocal(out=r_s[:, :], in_=s_s[:, :]), reads=[s_s], writes=[r_s])
                kb.op('dve', lambda e: e.tensor_tensor(out=q_n[:, 0:8, :], in0=p_q[:, :].rearrange("p (h d) -> p h d", d=64),
                                                       in1=r_s[:, 0:8].unsqueeze(2).to_broadcast([128, 8, 64]), op=ALU.mult),
                      reads=[p_q, r_s], writes=[q_n])
                kb.op('dve', lambda e: e.tensor_tensor(out=q_n[:, 8:10, :], in0=p_kv[:, 0:128].rearrange("p (h d) -> p h d", d=64),
                                                       in1=r_s[:, 8:10].unsqueeze(2).to_broadcast([128, 2, 64]), op=ALU.mult),
                      reads=[p_kv, r_s], writes=[q_n])
                kb.op('dve', lambda e: e.tensor_tensor(out=q_n[:, 0:8, :], in0=q_n[:, 0:8, :],
                                                       in1=gq[:, :].unsqueeze(1).to_broadcast([128, 8, 64]), op=ALU.mult),
                      reads=[q_n, gq], writes=[q_n])
                kb.op('dve', lambda e: e.tensor_tensor(out=q_n[:, 8:10, :], in0=q_n[:, 8:10, :],
                                                       in1=gk[:, :].unsqueeze(1).to_broadcast([128, 2, 64]), op=ALU.mult),
                      reads=[q_n, gk], writes=[q_n])
                qv = q_n[:, :, :].rearrange("p h (x a f) -> p h x a f", x=2, a=2)
                a_, b_ = qv[:, :, :, 0, :], qv[:, :, :, 1, :]
                cc = c_s[:, sub, 0:32].rearrange("p (x f) -> p x f", x=2).unsqueeze(1).to_broadcast([128, 10, 2, 16])
                sn = c_s[:, sub, 32:64].rearrange("p (x f) -> p x f", x=2).unsqueeze(1).to_broadcast([128, 10, 2, 16])
                q_r, t_1, t_2 = qr.next(), t1.next(), t2.next()
                qrv = q_r[:, :, :].rearrange("p h (x a f) -> p h x a f", x=2, a=2)
                kb.op('dve', lambda e: e.tensor_tensor(out=t_1[:, :, :, :], in0=a_, in1=cc, op=ALU.mult), reads=[q_n, c_s], writes=[t_1])
                kb.op('dve', lambda e: e.tensor_tensor(out=t_2[:, :, :, :], in0=b_, in1=sn, op=ALU.mult), reads=[q_n, c_s], writes=[t_2])
                kb.op('dve', lambda e: e.tensor_tensor(out=qrv[:, :, :, 0, :], in0=t_1[:, :, :, :], in1=t_2[:, :, :, :], op=ALU.subtract),
                      reads=[t_1, t_2], writes=[q_r])
                kb.op('dve', lambda e: e.tensor_tensor(out=t_1[:, :, :, :], in0=a_, in1=sn, op=ALU.mult), reads=[q_n, c_s], writes=[t_1])
                kb.op('dve', lambda e: e.tensor_tensor(out=t_2[:, :, :, :], in0=b_, in1=cc, op=ALU.mult), reads=[q_n, c_s], writes=[t_2])
                kb.op('dve', lambda e: e.tensor_tensor(out=qrv[:, :, :, 1, :], in0=t_1[:, :, :, :], in1=t_2[:, :, :, :], op=ALU.add),
                      reads=[t_1, t_2], writes=[q_r])
                q_b = qrb.next()
                kb.act(q_b, q_b[:, 0:8, :], q_r, q_r[:, 0:8, :], AF.Copy)
                kb.op('dve', lambda e: e.tensor_copy(out=q_b[:, 8:12, :].rearrange("p (g u) d -> p g u d", u=2),
                                                     in_=q_r[:, 8:10, :].unsqueeze(2).to_broadcast([128, 2, 2, 64])),
                      reads=[q_r], writes=[q_b])
                p_t = ptr.next()
                qbf = q_b[:, :, :].rearrange("p h d -> p (h d)")
                for c in range(6):
                    kb.op('pe', lambda e: e.transpose(p_t[:, c, :], qbf[:, c * 128:(c + 1) * 128], ident_bf[:, :]),
                          reads=[q_b, ident_bf], writes=[p_t])
                q_T = qT.next()
                kb.act(q_T, q_T[:, :, :], p_t, p_t[:, :, :], AF.Copy)
                tok = slice(r0 + sub * 128, r0 + (sub + 1) * 128)
                kb.dma('pool', S['QT'][:, tok].rearrange("(c p) t -> p c t", p=128), q_T[:, 0:4, :], reads=[q_T])
                kb.dma('pool', S['KTD'][:, tok].rearrange("(c p) t -> p c t", p=128), q_T[:, 4:6, :], reads=[q_T])
                v_a = va.next()
                kb.act(v_a, v_a[:, :, 0:64], p_kv, p_kv[:, 128:256].rearrange("p (h d) -> p h d", d=64), AF.Copy)
                kb.dma('pool', S['VA'][tok, :], v_a[:, :, :].rearrange("p h d -> p (h d)"), reads=[v_a])
            x_r, g_g = xrt.next(), ggt.next()
            for f in range(8):
                p_f = pf.next()
                for c in range(8):
                    kb.mm(p_f, p_f[:, :], w, w[:, c, 768 + f * 128:768 + (f + 1) * 128], h, h[:, c, :], start=(c == 0), stop=(c == 7))
                if f < 4:
                    kb.act(x_r, x_r[:, f, :], p_f, p_f[:, :], AF.Copy)
                else:
                    kb.act(g_g, g_g[:, f - 4, :], p_f, p_f[:, :], AF.Gelu_apprx_tanh)
            kb.dma('pool', S['XR'][:, r0:r0 + 512].rearrange("(c p) t -> p c t", p=128), x_r[:, :, :], reads=[x_r])
            kb.dma('pool', S['GG'][:, r0:r0 + 512].rearrange("(c p) t -> p c t", p=128), g_g[:, :, :], reads=[g_g])
        kb.barrier()
    kb.ph = None


def phase_e2(kb, P, C, S, j, seqs, rev):
    d = 1 if rev else 0
    with ExitStack() as ph:
        kb.ph = ph
        TC = 2048
        cw = kb.sb([128, 4, 4], F32, "cw")
        for jj in range(4):
            kb.dma('sp', cw[:, jj, :], P['b_conv_w'][j, jj].rearrange("(c p) -> p c", p=128), writes=[cw], slow=True)
        cb = load_col(kb, P['b_conv_b'][j], 4, "cb")
        br = load_col(kb, P['b_br'][j, d], 4, "br")
        bi = load_col(kb, P['b_bi'][j, d], 4, "bi")
        lam = load_col(kb, P['b_lambda'][j, d], 4, "lam")
        c1 = kb.sb([128, 4], F32, "c1")
        kb.act(c1, c1[:, :], lam, lam[:, :], AF.Exp, scale=-1.0)
        kb.act(c1, c1[:, :], c1, c1[:, :], AF.Ln, bias=1.0)
        kb.op('dve', lambda e: e.tensor_scalar(out=c1[:, :], in0=c1[:, :], scalar1=-8.0, scalar2=None, op0=ALU.mult),
              reads=[c1], writes=[c1])
        wr = kb.sb([128, 4, 128], F32, "wr")
        wi = kb.sb([128, 4, 128], F32, "wi")
        for wt, src in ((wr, P['b_wr']), (wi, P['b_wi'])):
            kb.op('dve', lambda e: e.memset(wt[:, :, :], 0.0), writes=[wt])
            for cc in range(4):
                for b in range(2):
                    kb.dma('sp', wt[b * 64:(b + 1) * 64, cc, b * 64:(b + 1) * 64], src[j, d, 2 * cc + b], writes=[wt])
        xrt = Rot([kb.sb([128, TC + 3], F32, "xrt") for _ in range(2)])
        xc = Rot([kb.sb([128, TC], F32, "xc") for _ in range(2)])
        ga = Rot([kb.sb([128, TC], F32, "ga") for _ in range(2)])
        gm = Rot([kb.sb([128, TC], F32, "gm") for _ in range(2)])
        gu = Rot([kb.sb([128, TC], F32, "gu") for _ in range(2)])
        hh = Rot([kb.sb([128, TC], F32, "hh") for _ in range(2)])
        hf = Rot([kb.sb([128, TC], F32, "hf") for _ in range(2)])
        gg = Rot([kb.sb([128, TC], F32, "gg") for _ in range(2)])
        yb = Rot([kb.sb([128, TC], BF16, "yb") for _ in range(2)])
        carry = Rot([kb.sb([128, 1], F32, "carry") for _ in range(2)])
        pg = Rot([kb.ps([128, 512], F32, "pg") for _ in range(4)])
        t0 = 0
        for Sq in seqs:
            tc = min(TC, Sq)
            nchunk = Sq // tc
            for cc in range(4):
                rows = slice(cc * 128, (cc + 1) * 128)
                cr = None
                order = range(nchunk - 1, -1, -1) if rev else range(nchunk)
                for ic in order:
                    a0 = t0 + ic * tc
                    x_r = xrt.next()
                    lo = 2 if ic > 0 else 0
                    hi = 1 if ic < nchunk - 1 else 0
                    if lo == 0:
                        kb.op('dve', lambda e: e.memset(x_r[:, 0:2], 0.0), writes=[x_r])
                    if hi == 0:
                        kb.op('dve', lambda e: e.memset(x_r[:, tc + 2:tc + 3], 0.0), writes=[x_r])
                    kb.dma('sp', x_r[:, 2 - lo:tc + 2 + hi], S['XR'][rows, a0 - lo:a0 + tc + hi], writes=[x_r])
                    x_c = xc.next()
                    kb.op('dve', lambda e: e.tensor_scalar(out=x_c[:, 0:tc], in0=x_r[:, 0:tc], scalar1=cw[:, 0, cc:cc + 1],
                                                           scalar2=cb[:, cc:cc + 1], op0=ALU.mult, op1=ALU.add),
                          reads=[x_r, cw, cb], writes=[x_c])
                    for jj in range(1, 4):
                        kb.op('dve', lambda e: e.scalar_tensor_tensor(out=x_c[:, 0:tc], in0=x_r[:, jj:jj + tc], scalar=cw[:, jj, cc:cc + 1],
                                                                      in1=x_c[:, 0:tc], op0=ALU.mult, op1=ALU.add),
                              reads=[x_r, cw, x_c], writes=[x_c])
                    g_a, g_m, g_u = ga.next(), gm.next(), gu.next()
                    for n in range(tc // 512):
                        cols = slice(n * 512, (n + 1) * 512)
                        p_r, p_i = pg.next(), pg.next()
                        kb.mm(p_r, p_r[:, :], wr, wr[:, cc, :], x_c, x_c[:, cols])
                        kb.mm(p_i, p_i[:, :], wi, wi[:, cc, :], x_c, x_c[:, cols])
                        kb.act(g_a, g_a[:, cols], p_r, p_r[:, :], AF.Sigmoid, bias=br[:, cc:cc + 1], extra_reads=[br])
                        kb.act(g_u, g_u[:, cols], p_i, p_i[:, :], AF.Sigmoid, bias=bi[:, cc:cc + 1], extra_reads=[bi])
                    kb.act(g_a, g_a[:, 0:tc], g_a, g_a[:, 0:tc], AF.Exp, scale=c1[:, cc:cc + 1], extra_reads=[c1])
                    kb.act(g_m, g_m[:, 0:tc], g_a, g_a[:, 0:tc], AF.Square)
                    kb.act(g_m, g_m[:, 0:tc], g_m, g_m[:, 0:tc], AF.Sqrt, scale=-1.0, bias=1.0)
                    kb.op('dve', lambda e: e.tensor_tensor(out=g_u[:, 0:tc], in0=g_u[:, 0:tc], in1=x_c[:, 0:tc], op=ALU.mult),
                          reads=[g_u, x_c], writes=[g_u])
                    kb.op('dve', lambda e: e.tensor_tensor(out=g_u[:, 0:tc], in0=g_u[:, 0:tc], in1=g_m[:, 0:tc], op=ALU.mult),
                          reads=[g_u, g_m], writes=[g_u])
                    h_ = hh.next()
                    init = 0.0 if cr is None else cr[:, 0:1]
                    rd = [g_a, g_u] + ([] if cr is None else [cr])
                    if rev:
                        kb.op('dve', lambda e: e.tensor_tensor_scan(out=h_[:, tc - 1::-1] if False else h_[:, 0:tc][:, ::-1],
                                                                    data0=g_a[:, 0:tc][:, ::-1], data1=g_u[:, 0:tc][:, ::-1],
                                                                    initial=init, op0=ALU.mult, op1=ALU.add), reads=rd, writes=[h_])
                    else:
                        kb.op('dve', lambda e: e.tensor_tensor_scan(out=h_[:, 0:tc], data0=g_a[:, 0:tc], data1=g_u[:, 0:tc],
                                                                    initial=init, op0=ALU.mult, op1=ALU.add), reads=rd, writes=[h_])
                    cr = carry.next()
                    edge = 0 if rev else tc - 1
                    kb.op('dve', lambda e: e.tensor_copy(out=cr[:, :], in_=h_[:, edge:edge + 1]), reads=[h_], writes=[cr])
                    if not rev:
                        kb.dma('pool', S['HF'][rows, a0:a0 + tc], h_[:, 0:tc], reads=[h_])
                    else:
                        h_f, g_g, y_b = hf.next(), gg.next(), yb.next()
                        kb.dma('sp', h_f[:, 0:tc], S['HF'][rows, a0:a0 + tc], writes=[h_f])
                        kb.dma('sp', g_g[:, 0:tc], S['GG'][rows, a0:a0 + tc], writes=[g_g])
                        kb.op('dve', lambda e: e.tensor_tensor(out=h_[:, 0:tc], in0=h_[:, 0:tc], in1=h_f[:, 0:tc], op=ALU.add),
                              reads=[h_, h_f], writes=[h_])
                        kb.op('dve', lambda e: e.tensor_tensor(out=y_b[:, 0:tc], in0=h_[:, 0:tc], in1=g_g[:, 0:tc], op=ALU.mult),
                              reads=[h_, g_g], writes=[y_b])
                        kb.dma('pool', S['YB'][rows, a0:a0 + tc], y_b[:, 0:tc], reads=[y_b])
            t0 += Sq
        kb.barrier()
    kb.ph = None


def phase_e3(kb, P, C, S, seqs):
    with ExitStack() as ph:
        kb.ph = ph
        Smax = max(seqs)
        kt_t = kb.sb([128, 2, Smax], BF16, "kt")
        va_t = kb.sb([128, Smax // 128, 130], BF16, "vat")
        ones = kb.sb([128, 64], F32, "ones")
        kb.op('dve', lambda e: e.memset(ones[:, :], 1.0), writes=[ones])
        qt = Rot([kb.sb([128, 4, 512], BF16, "qt") for _ in range(2)])
        pe_ = Rot([kb.sb([128, 512], BF16, "pexp") for _ in range(4)])
        num = Rot([kb.sb([65, 512], F32, "num") for _ in range(2)])
        rec = Rot([kb.sb([64, 512], F32, "rec") for _ in range(2)])
        yt = Rot([kb.sb([64, 512], BF16, "yt") for _ in range(3)])
        pss = Rot([kb.ps([128, 512], F32, "pss") for _ in range(4)])
        pac = Rot([kb.ps([65, 512], F32, "pac") for _ in range(2)])
        pdn = Rot([kb.ps([64, 512], F32, "pdn") for _ in range(1)])
        t0 = 0
        for Sq in seqs:
            nkt = Sq // 128
            for c0 in range(0, Sq, 1024):
                for g_ in range(2):
                    kb.dma('sp', kt_t[:, g_, c0:c0 + 1024], S['KTD'][g_ * 128:(g_ + 1) * 128, t0 + c0:t0 + c0 + 1024], writes=[kt_t])
                kb.dma('sp', va_t[:, c0 // 128:c0 // 128 + 8, :], S['VA'][t0 + c0:t0 + c0 + 1024, :].rearrange("(k p) f -> p k f", p=128), writes=[va_t])
            for qi in range(Sq // 512):
                q0 = t0 + qi * 512
                q_t = qt.next()
                kb.dma('sp', q_t[:, :, :], S['QT'][:, q0:q0 + 512].rearrange("(c p) t -> p c t", p=128), writes=[q_t])
                for h in range(8):
                    ch, pb, g = h // 2, (h % 2) * 64, h // 4
                    acc = pac.next()
                    for k in range(nkt):
                        p_s = pss.next()
                        kb.mm(p_s, p_s[:, :], kt_t, kt_t[pb:pb + 64, g, k * 128:(k + 1) * 128], q_t, q_t[pb:pb + 64, ch, :])
                        p_e = pe_.next()
                        kb.act(p_e, p_e[:, :], p_s, p_s[:, :], AF.Exp)
                        kb.mm(acc, acc[:, :], va_t, va_t[:, k, g * 65:(g + 1) * 65], p_e, p_e[:, :], start=(k == 0), stop=(k == nkt - 1))
                    n_, r_, y_, p_d = num.next(), rec.next(), yt.next(), pdn.next()
                    kb.act(n_, n_[:, :], acc, acc[:, :], AF.Copy)
                    kb.mm(p_d, p_d[:, :], ones, ones[64:65, 0:64], n_, n_[64:65, :])
                    kb.op('dve', lambda e: e.reciprocal(out=r_[:, :], in_=p_d[:, :]), reads=[p_d], writes=[r_])
                    kb.op('dve', lambda e: e.tensor_tensor(out=y_[:, :], in0=n_[0:64, :], in1=r_[:, :], op=ALU.mult),
                          reads=[n_, r_], writes=[y_])
                    kb.dma('pool', S['YA'][h * 64:(h + 1) * 64, q0:q0 + 512], y_[:, :], reads=[y_])
            t0 += Sq
        kb.barrier()
    kb.ph = None


def phase_outproj(kb, P, C, S, w_ap, srcs, x_src, x_dst, T):
    with ExitStack() as ph:
        kb.ph = ph
        nch = sum(n for _, n in srcs)
        w = load_w_bf16(kb, w_ap, nch * 128, D, "wout")
        xs = Rot([kb.sb([128, 4, D], F32, "x") for _ in range(2)])
        yT = Rot([kb.sb([128, nch, 512], BF16, "yT") for _ in range(2)])
        pso = Rot([kb.ps([128, 512], F32, "pso") for _ in range(4)])
        for grp in range(T // 512):
            r0 = grp * 512
            x, y_ = xs.next(), yT.next()
            kb.dma('sp', x[:, :, :], x_src[r0:r0 + 512, :].rearrange("(s p) d -> p s d", p=128), writes=[x])
            c0 = 0
            for ap_, n in srcs:
                kb.dma('sp', y_[:, c0:c0 + n, :], ap_[:, r0:r0 + 512].rearrange("(c p) t -> p c t", p=128), writes=[y_])
                c0 += n
            for sub in range(4):
                for half in range(2):
                    p_o = pso.next()
                    for c in range(nch):
                        kb.mm(p_o, p_o[:, :], y_, y_[:, c, sub * 128:(sub + 1) * 128], w, w[:, c, half * 512:(half + 1) * 512],
                              start=(c == 0), stop=(c == nch - 1))
                    kb.op('dve', lambda e: e.tensor_tensor(out=x[:, sub, half * 512:(half + 1) * 512],
                                                           in0=x[:, sub, half * 512:(half + 1) * 512], in1=p_o[:, :], op=ALU.add),
                          reads=[x, p_o], writes=[x])
            kb.dma('pool', x_dst[r0:r0 + 512, :].rearrange("(s p) d -> p s d", p=128), x[:, :, :], reads=[x])
        kb.barrier()
    kb.ph = None


def phase_o1(kb, P, C, S, x_src, j, layer, T):
    with ExitStack() as ph:
        kb.ph = ph
        ident_bf, ident_f, epsb = make_consts(kb, C)
        w = load_w_bf16(kb, P['w_in_o'][j], D, 4368, "wino")
        g = load_col(kb, P['norm_mix'][layer], 8, "gmix")
        nexpa = load_bcast(kb, P['c_a_log'][j].rearrange("a b -> (a b)"), 8, "nexpa")
        dtb = load_bcast(kb, P['c_dt_bias'][j].rearrange("a b -> (a b)"), 8, "dtb")
        kb.act(nexpa, nexpa[:, :], nexpa, nexpa[:, :], AF.Exp)
        kb.op('dve', lambda e: e.tensor_scalar(out=nexpa[:, :], in0=nexpa[:, :], scalar1=-1.0, scalar2=None, op0=ALU.mult),
              reads=[nexpa], writes=[nexpa])
        nt = NormT(kb, g, ident_bf)
        nt.epsb = epsb
        xs = Rot([kb.sb([128, 4, D], F32, "x") for _ in range(2)])
        hT = Rot([kb.sb([128, 8, 512], BF16, "hT") for _ in range(2)])
        pf = Rot([kb.ps([128, 512], F32, "pf") for _ in range(4)])
        pb = Rot([kb.ps([128, 16], F32, "pb") for _ in range(1)])
        qkv = Rot([kb.sb([128, 12, 512], F32, "qkv") for _ in range(1)])
        dqk = Rot([kb.sb([128, 12, 512], BF16, "dqk") for _ in range(1)])
        zs = Rot([kb.sb([128, 4, 512], F32, "zs") for _ in range(1)])
        bg = Rot([kb.sb([128, 4, 16], F32, "bg") for _ in range(2)])
        tmp = Rot([kb.sb([128, 8], F32, "tmp") for _ in range(2)])
        dv = Rot([kb.sb([128, 12, 65], BF16, "dv") for _ in range(2)])
        for v in dv.t:
            kb.op('dve', lambda e: e.memset(v[:, :, :], 1.0), writes=[v])
        for grp in range(T // 512):
            x, h = xs.next(), hT.next()
            r0 = grp * 512
            kb.dma('sp', x[:, :, :], x_src[r0:r0 + 512, :].rearrange("(s p) d -> p s d", p=128), writes=[x])
            for sub in range(4):
                nt(x, x[:, sub, :], h, h[:, :, sub * 128:(sub + 1) * 128])
            q_ = qkv.next()
            for f in range(12):
                p_f = pf.next()
                for c in range(8):
                    kb.mm(p_f, p_f[:, :], w, w[:, c, f * 128:(f + 1) * 128], h, h[:, c, :], start=(c == 0), stop=(c == 7))
                kb.act(q_, q_[:, f, :], p_f, p_f[:, :], AF.Copy)
            kb.dma('pool', S['QKV'][:, r0:r0 + 512].rearrange("(c p) t -> p c t", p=128), q_[:, :, :], reads=[q_])
            d_ = dqk.next()
            for f in range(12):
                p_f = pf.next()
                c0 = 2064 + f * 128
                for c in range(8):
                    kb.mm(p_f, p_f[:, :], w, w[:, c, c0:c0 + 128], h, h[:, c, :], start=(c == 0), stop=(c == 7))
                kb.act(d_, d_[:, f, :], p_f, p_f[:, :], AF.Copy, scale=(0.125 if f < 6 else 1.0))
            kb.dma('pool', S['DQK'][:, r0:r0 + 512].rearrange("(c p) t -> p c t", p=128), d_[:, :, :], reads=[d_])
            z_, b_ = zs.next(), bg.next()
            for sub in range(4):
                hs = slice(sub * 128, (sub + 1) * 128)
                p_f = pf.next()
                for c in range(8):
                    kb.mm(p_f, p_f[:, :], h, h[:, c, hs], w, w[:, c, 1536:2048], start=(c == 0), stop=(c == 7))
                kb.act(z_, z_[:, sub, :], p_f, p_f[:, :], AF.Silu)
                p_b = pb.next()
                for c in range(8):
                    kb.mm(p_b, p_b[:, :], h, h[:, c, hs], w, w[:, c, 2048:2064], start=(c == 0), stop=(c == 7))
                kb.act(b_, b_[:, sub, 0:8], p_b, p_b[:, 0:8], AF.Sigmoid)
                t_ = tmp.next()
                kb.op('dve', lambda e: e.tensor_tensor(out=t_[:, :], in0=p_b[:, 8:16], in1=dtb[:, :], op=ALU.add),
                      reads=[p_b, dtb], writes=[t_])
                kb.act(t_, t_[:, :], t_, t_[:, :], AF.Exp)
                kb.act(t_, t_[:, :], t_, t_[:, :], AF.Ln, bias=1.0)
                kb.op('dve', lambda e: e.tensor_tensor(out=b_[:, sub, 8:16], in0=t_[:, :], in1=nexpa[:, :], op=ALU.mult),
                      reads=[t_, nexpa], writes=[b_])
                v_ = dv.next()
                for part, (a, n) in enumerate(((3600, 512), (4112, 256))):
                    p_f = pf.next()
                    for c in range(8):
                        kb.mm(p_f, p_f[:, 0:n], h, h[:, c, hs], w, w[:, c, a:a + n], start=(c == 0), stop=(c == 7))
                    nh = n // 64
                    kb.act(v_, v_[:, part * 8:part * 8 + nh, 0:64], p_f, p_f[:, 0:n].rearrange("p (h d) -> p h d", d=64), AF.Copy)
                tok = slice(r0 + sub * 128, r0 + (sub + 1) * 128)
                kb.dma('pool', S['DV'][tok, :], v_[:, :, :].rearrange("p h d -> p (h d)"), reads=[v_])
            kb.dma('pool', S['ZS'][r0:r0 + 512, :].rearrange("(s p) d -> p s d", p=128), z_[:, :, :], reads=[z_])
            kb.dma('pool', S['BG'][r0:r0 + 512, :].rearrange("(s p) d -> p s d", p=128), b_[:, :, :], reads=[b_])
        kb.barrier()
    kb.ph = None


def phase_o2a(kb, P, C, S, j, seqs):
    with ExitStack() as ph:
        kb.ph = ph
        ident_bf, ident_f, epsb = make_consts(kb, C)
        cw = kb.sb([128, 4, 12], F32, "cw")
        for jj in range(4):
            kb.dma('sp', cw[:, jj, :], P['c_conv_w'][j, jj].rearrange("(c p) -> p c", p=128), writes=[cw], slow=True)
        ones = kb.sb([128, 128], F32, "ones")
        kb.op('dve', lambda e: e.memset(ones[:, :], 1.0), writes=[ones])
        xin = Rot([kb.sb([128, 12, 515], F32, "xin") for _ in range(2)])
        xc = Rot([kb.sb([128, 12, 512], F32, "xc") for _ in range(2)])
        sq = Rot([kb.sb([128, 512], F32, "sq") for _ in range(2)])
        rs = Rot([kb.sb([128, 512], F32, "rs") for _ in range(2)])
        kv = Rot([kb.sb([128, 4, 4, 256], F32, "kv") for _ in range(1)])
        pn = Rot([kb.ps([128, 512], F32, "pn") for _ in range(2)])
        pt = Rot([kb.ps([128, 4, 128], F32, "pt") for _ in range(2)])
        t0 = 0
        for Sq in seqs:
            for blk in range(Sq // 512):
                a0 = t0 + blk * 512
                x_ = xin.next()
                lo = 2 if blk > 0 else 0
                hi = 1 if blk < Sq // 512 - 1 else 0
                if lo == 0:
                    kb.op('dve', lambda e: e.memset(x_[:, :, 0:2], 0.0), writes=[x_])
                if hi == 0:
                    kb.op('dve', lambda e: e.memset(x_[:, :, 514:515], 0.0), writes=[x_])
                kb.dma('sp', x_[:, :, 2 - lo:514 + hi], S['QKV'][:, a0 - lo:a0 + 512 + hi].rearrange("(c p) t -> p c t", p=128), writes=[x_])
                c_ = xc.next()
                for f in range(12):
                    kb.op('dve', lambda e: e.tensor_scalar(out=c_[:, f, :], in0=x_[:, f, 0:512], scalar1=cw[:, 0, f:f + 1], scalar2=None, op0=ALU.mult),
                          reads=[x_, cw], writes=[c_])
                    for jj in range(1, 4):
                        kb.op('dve', lambda e: e.scalar_tensor_tensor(out=c_[:, f, :], in0=x_[:, f, jj:jj + 512], scalar=cw[:, jj, f:f + 1],
                                                                      in1=c_[:, f, :], op0=ALU.mult, op1=ALU.add),
                              reads=[x_, cw, c_], writes=[c_])
                kb.act(c_, c_[:, :, :], c_, c_[:, :, :], AF.Silu)
                for f in range(8):
                    s_, r_, p_n = sq.next(), rs.next(), pn.next()
                    kb.act(s_, s_[:, :], c_, c_[:, f, :], AF.Square)
                    kb.mm(p_n, p_n[:, :], ones, ones[:, :], s_, s_[:, :])
                    kb.act(s_, s_[:, :], p_n, p_n[:, :], AF.Sqrt, bias=epsb[:, :], extra_reads=[epsb])
                    kb.op('dve', lambda e: e.reciprocal(out=r_[:, :], in_=s_[:, :]), reads=[s_], writes=[r_])
                    if f < 4:
                        kb.op('dve', lambda e: e.scalar_tensor_tensor(out=c_[:, f, :], in0=c_[:, f, :], scalar=float(128 ** -0.5), in1=r_[:, :],
                                                                      op0=ALU.mult, op1=ALU.mult), reads=[c_, r_], writes=[c_])
                    else:
                        kb.op('dve', lambda e: e.tensor_tensor(out=c_[:, f, :], in0=c_[:, f, :], in1=r_[:, :], op=ALU.mult),
                              reads=[c_, r_], writes=[c_])
                kb.dma('pool', S['CQK'][:, a0:a0 + 512].rearrange("(c p) t -> p c t", p=128), c_[:, 0:8, :], reads=[c_])
                k_ = kv.next()
                for f in range(4, 12):
                    hd, kvi = (f - 4) % 4, (f - 4) // 4
                    p_t = pt.next()
                    for sub in range(4):
                        kb.op('pe', lambda e: e.transpose(p_t[:, sub, :], c_[:, f, sub * 128:(sub + 1) * 128], ident_f[:, :]),
                              reads=[c_, ident_f], writes=[p_t])
                    kb.act(k_, k_[:, :, hd, kvi * 128:(kvi + 1) * 128], p_t, p_t[:, :, :], AF.Copy)
                kb.dma('pool', S['CKV'][a0:a0 + 512, :].rearrange("(s p) f -> p s f", p=128), k_[:, :, :, :].rearrange("p s h f -> p s (h f)"), reads=[k_])
            t0 += Sq
        kb.barrier()
    kb.ph = None


def phase_o2b(kb, P, C, S, seqs, rev):
    d = 1 if rev else 0
    with ExitStack() as ph:
        kb.ph = ph
        ident_bf, ident_f, epsb = make_consts(kb, C)
        BA = kb.sb([128, 128], F32, "BA")
        MS = kb.sb([128, 128], F32, "MS")
        kb.dma('sp', BA[:, :], C['ba'][d], writes=[BA])
        kb.dma('sp', MS[:, :], C['ms'][d], writes=[MS])
        ones = kb.sb([128, 128], F32, "ones")
        kb.op('dve', lambda e: e.memset(ones[:, :], 1.0), writes=[ones])
        St = [kb.sb([128, 128], F32, "state") for _ in range(4)]
        qk = Rot([kb.sb([128, 8, 128], F32, "qk") for _ in range(2)])
        kvt = Rot([kb.sb([128, 4, 256], F32, "kvt") for _ in range(2)])
        bgt = Rot([kb.sb([128, 16], F32, "bgt") for _ in range(2)])
        sc = Rot([kb.sb([128, 5, 4], F32, "sc") for _ in range(2)])
        ot = Rot([kb.sb([128, 4, 128], F32, "ot") for _ in range(2)])
        oft = Rot([kb.sb([128, 4, 128], F32, "oft") for _ in range(2)])
        NB = 6
        pbank = [kb.ps([128, 4, 128], F32, "pb") for _ in range(NB)]
        pp = Rot([View(b, b[:, k, :]) for k in range(4) for b in pbank])
        pg = Rot([kb.ps([128, 16], F32, "pg") for _ in range(2)])

        def mk(name, n):
            return [Rot([kb.sb([128, 128], F32, name) for _ in range(n)]) for _ in range(4)]
        gY, Dm, DTm, Pa, Pta, Rr, ru, rw, uu, wT, AT, vn, o2, Kp = [mk(nm, 2) for nm in
            ("gY", "Dm", "DTm", "Pa", "Pta", "R", "ru", "rw", "u", "wT", "AT", "vn", "o2", "Kp")]
        extra_names = ("Rt", "M0", "M0t", "X", "Xt", "C1", "C1t", "C2", "C2t", "C3", "C3t", "C4t")
        extra = {nm: mk(nm, 2) for nm in extra_names}
        KM = kb.sb([128, 5, 128], F32, "KM")
        kb.dma('sp', KM[:, :, :], C['km'].rearrange("k p f -> p k f"), writes=[KM])
        t0 = 0
        for Sq in seqs:
            nch = Sq // 128
            for hd in range(4):
                kb.op('dve', lambda e: e.memset(St[hd][:, :], 0.0), writes=[St[hd]])
            for ic in (range(nch - 1, -1, -1) if rev else range(nch)):
                a0 = t0 + ic * 128
                q_, k_, b_, s_ = qk.next(), kvt.next(), bgt.next(), sc.next()
                kb.dma('sp', q_[:, :, :], S['CQK'][:, a0:a0 + 128].rearrange("(c p) t -> p c t", p=128), writes=[q_])
                kb.dma('sp', k_[:, :, :], S['CKV'][a0:a0 + 128, :].rearrange("p (h f) -> p h f", f=256), writes=[k_])
                kb.dma('sp', b_[:, :], S['BG'][a0:a0 + 128, :], writes=[b_])
                beta = b_[:, d * 4:d * 4 + 4]
                gcol = b_[:, 8 + d * 4:8 + d * 4 + 4]
                p_g = pg.next()
                kb.mm(p_g, p_g[:, 0:4], BA, BA[:, :], b_, gcol)
                kb.mm(p_g, p_g[:, 4:8], ones, ones[:, :], b_, gcol)
                kb.op('dve', lambda e: e.tensor_scalar(out=s_[:, 0, :], in0=beta, scalar1=-1.0, scalar2=None, op0=ALU.mult), reads=[b_], writes=[s_])
                kb.act(s_, s_[:, 1, :], p_g, p_g[:, 0:4], AF.Exp)
                kb.op('dve', lambda e: e.tensor_tensor(out=s_[:, 2, :], in0=p_g[:, 4:8], in1=p_g[:, 0:4], op=ALU.subtract) if False else
                      e.tensor_copy(out=s_[:, 2, :], in_=p_g[:, 0:4]), reads=[p_g], writes=[s_])
                kb.op('dve', lambda e: e.tensor_tensor(out=s_[:, 2, :], in0=p_g[:, 4:8], in1=s_[:, 2, :], op=ALU.subtract), reads=[p_g, s_], writes=[s_])
                kb.act(s_, s_[:, 2, :], s_, s_[:, 2, :], AF.Exp)
                kb.act(s_, s_[:, 3, :], p_g, p_g[:, 4:8], AF.Exp)
                kb.op('dve', lambda e: e.tensor_tensor(out=s_[:, 4, :], in0=beta, in1=s_[:, 1, :], op=ALU.mult), reads=[b_, s_], writes=[s_])
                o_ = ot.next()
                if O2B_LVL < 2:
                    kb.op('dve', lambda e: e.memset(o_[:, :, :], 0.0), writes=[o_])
                    kb.op('dve', lambda e: e.tensor_copy(out=o_[:, 0, 0:20], in_=s_[:, :, :].rearrange("p a b -> p (a b)")), reads=[s_], writes=[o_])
                    kb.dma('pool', S['OF'][a0:a0 + 128, :], o_[:, :, :].rearrange("p h f -> p (h f)"), reads=[o_])
                    continue
                T_ = {}
                for hd in range(4):
                    T_[hd] = dict(gY=gY[hd].next(), Dm=Dm[hd].next(), DTm=DTm[hd].next(), P=Pa[hd].next(), Pt=Pta[hd].next(),
                                  R=Rr[hd].next(), ru=ru[hd].next(), rw=rw[hd].next(),
                                  u=uu[hd].next(), wT=wT[hd].next(), AT=AT[hd].next(), vn=vn[hd].next(), o2=o2[hd].next(), Kp=Kp[hd].next())
                    for nm in extra_names:
                        T_[hd][nm] = extra[nm][hd].next()
                for hd in range(4):
                    t = T_[hd]
                    kT = q_[:, 4 + hd, :]
                    kb.op('dve', lambda e: e.tensor_scalar(out=t['gY'][:, :], in0=MS[:, :], scalar1=b_[:, 8 + d * 4 + hd:9 + d * 4 + hd], scalar2=None, op0=ALU.mult),
                          reads=[MS, b_], writes=[t['gY']])
                    p1, p2, p3 = pp.next(), pp.next(), pp.next()
                    kb.mm(p1, p1[:, :], BA, BA[:, :], t['gY'], t['gY'][:, :])
                    kb.mm(p2, p2[:, :], t['gY'], t['gY'][:, :], BA, BA[:, :])
                    kb.mm(p3, p3[:, :], q_, kT, q_, kT)
                    kb.act(t['Dm'], t['Dm'][:, :], p1, p1[:, :], AF.Exp)
                    kb.act(t['DTm'], t['DTm'][:, :], p2, p2[:, :], AF.Exp)
                    kb.op('dve', lambda e: e.tensor_tensor(out=t['Dm'][:, :], in0=t['Dm'][:, :], in1=MS[:, :], op=ALU.mult), reads=[t['Dm'], MS], writes=[t['Dm']])
                    kb.op('dve', lambda e: e.tensor_tensor(out=t['DTm'][:, :], in0=t['DTm'][:, :], in1=BA[:, :], op=ALU.mult), reads=[t['DTm'], BA], writes=[t['DTm']])
                    kb.op('dve', lambda e: e.scalar_tensor_tensor(out=t['Pt'][:, :], in0=p3[:, :], scalar=s_[:, 0, hd:hd + 1], in1=t['Dm'][:, :],
                                                                  op0=ALU.mult, op1=ALU.mult), reads=[p3, s_, t['Dm']], writes=[t['Pt']])
                    p4 = pp.next()
                    kb.op('pe', lambda e: e.transpose(p4[:, :], t['Pt'][:, :], ident_f[:, :]), reads=[t['Pt'], ident_f], writes=[p4])
                    kb.act(t['P'], t['P'][:, :], p4, p4[:, :], AF.Copy)
                if O2B_LVL < 3:
                    for hd in range(4):
                        kb.op('dve', lambda e: e.tensor_copy(out=o_[:, hd, :], in_=T_[hd]['R'][:, :]), reads=[T_[hd]['R']], writes=[o_])
                    kb.dma('pool', S['OF'][a0:a0 + 128, :], o_[:, :, :].rearrange("p h f -> p (h f)"), reads=[o_])
                    continue
                for hd in range(4):
                    t = T_[hd]
                    for nm, src, ki in (("M0", 'P', 0), ("M0t", 'Pt', 0), ("C1", 'P', 1), ("C1t", 'Pt', 1), ("C2", 'P', 2), ("C2t", 'Pt', 2),
                                        ("C3", 'P', 3), ("C3t", 'Pt', 3), ("C4", 'P', 4), ("C4t", 'Pt', 4)):
                        if nm == "C4":
                            continue
                        kb.op('dve', lambda e: e.tensor_tensor(out=t[nm][:, :], in0=t[src][:, :], in1=KM[:, ki, :], op=ALU.mult),
                              reads=[t[src], KM], writes=[t[nm]])
                    kb.op('dve', lambda e: e.tensor_tensor(out=t['R'][:, :], in0=t['M0'][:, :], in1=ident_f[:, :], op=ALU.add), reads=[t['M0'], ident_f], writes=[t['R']])
                    kb.op('dve', lambda e: e.tensor_tensor(out=t['Rt'][:, :], in0=t['M0t'][:, :], in1=ident_f[:, :], op=ALU.add), reads=[t['M0t'], ident_f], writes=[t['Rt']])
                for st in range(2):
                    for hd in range(4):
                        t = T_[hd]
                        pa, pb_ = pp.next(), pp.next()
                        kb.mm(pa, pa[:, :], t['M0t'], t['M0t'][:, :], t['M0'], t['M0'][:, :])
                        kb.mm(pb_, pb_[:, :], t['M0'], t['M0'][:, :], t['M0t'], t['M0t'][:, :])
                        kb.act(t['X'], t['X'][:, :], pa, pa[:, :], AF.Copy)
                        kb.act(t['Xt'], t['Xt'][:, :], pb_, pb_[:, :], AF.Copy)
                        pc, pd = pp.next(), pp.next()
                        kb.mm(pc, pc[:, :], t['Xt'], t['Xt'][:, :], t['R'], t['R'][:, :])
                        kb.mm(pd, pd[:, :], t['X'], t['X'][:, :], t['Rt'], t['Rt'][:, :])
                        kb.op('dve', lambda e: e.tensor_tensor(out=t['R'][:, :], in0=t['R'][:, :], in1=pc[:, :], op=ALU.add), reads=[t['R'], pc], writes=[t['R']])
                        kb.op('dve', lambda e: e.tensor_tensor(out=t['Rt'][:, :], in0=t['Rt'][:, :], in1=pd[:, :], op=ALU.add), reads=[t['Rt'], pd], writes=[t['Rt']])
                        t['M0'], t['X'] = t['X'], t['M0']
                        t['M0t'], t['Xt'] = t['Xt'], t['M0t']
                for lv in range(1, 5):
                    for hd in range(4):
                        t = T_[hd]
                        cm, cmt = t.get('C%d' % lv), t['C%dt' % lv]
                        pa = pp.next()
                        kb.mm(pa, pa[:, :], cmt, cmt[:, :], t['R'], t['R'][:, :])
                        kb.act(t['X'], t['X'][:, :], pa, pa[:, :], AF.Copy)
                        if lv < 4:
                            pb_ = pp.next()
                            kb.mm(pb_, pb_[:, :], cm, cm[:, :], t['Rt'], t['Rt'][:, :])
                            kb.act(t['Xt'], t['Xt'][:, :], pb_, pb_[:, :], AF.Copy)
                        pc = pp.next()
                        kb.mm(pc, pc[:, :], t['Rt'], t['Rt'][:, :], t['X'], t['X'][:, :])
                        if lv < 4:
                            pd = pp.next()
                            kb.mm(pd, pd[:, :], t['R'], t['R'][:, :], t['Xt'], t['Xt'][:, :])
                        kb.op('dve', lambda e: e.tensor_tensor(out=t['R'][:, :], in0=t['R'][:, :], in1=pc[:, :], op=ALU.add), reads=[t['R'], pc], writes=[t['R']])
                        if lv < 4:
                            kb.op('dve', lambda e: e.tensor_tensor(out=t['Rt'][:, :], in0=t['Rt'][:, :], in1=pd[:, :], op=ALU.add), reads=[t['Rt'], pd], writes=[t['Rt']])
                if O2B_LVL < 4:
                    for hd in range(4):
                        kb.op('dve', lambda e: e.tensor_copy(out=o_[:, hd, :], in_=T_[hd]['R'][:, :]), reads=[T_[hd]['R']], writes=[o_])
                    kb.dma('pool', S['OF'][a0:a0 + 128, :], o_[:, :, :].rearrange("p h f -> p (h f)"), reads=[o_])
                    continue
                for hd in range(4):
                    t = T_[hd]
                    qT, kT = q_[:, hd, :], q_[:, 4 + hd, :]
                    Kt, Vt = k_[:, hd, 0:128], k_[:, hd, 128:256]
                    kb.op('dve', lambda e: e.tensor_scalar(out=t['ru'][:, :], in0=Vt, scalar1=b_[:, d * 4 + hd:d * 4 + hd + 1], scalar2=None, op0=ALU.mult),
                          reads=[k_, b_], writes=[t['ru']])
                    kb.op('dve', lambda e: e.tensor_scalar(out=t['rw'][:, :], in0=Kt, scalar1=s_[:, 4, hd:hd + 1], scalar2=None, op0=ALU.mult),
                          reads=[k_, s_], writes=[t['rw']])
                    kb.op('dve', lambda e: e.tensor_scalar(out=t['Kp'][:, :], in0=Kt, scalar1=s_[:, 2, hd:hd + 1], scalar2=None, op0=ALU.mult),
                          reads=[k_, s_], writes=[t['Kp']])
                    p1, p2, p3 = pp.next(), pp.next(), pp.next()
                    kb.mm(p1, p1[:, :], t['R'], t['R'][:, :], t['ru'], t['ru'][:, :])
                    kb.mm(p2, p2[:, :], t['rw'], t['rw'][:, :], t['R'], t['R'][:, :])
                    kb.mm(p3, p3[:, :], q_, kT, q_, qT)
                    kb.act(t['u'], t['u'][:, :], p1, p1[:, :], AF.Copy)
                    kb.act(t['wT'], t['wT'][:, :], p2, p2[:, :], AF.Copy)
                    kb.op('dve', lambda e: e.tensor_tensor(out=t['AT'][:, :], in0=p3[:, :], in1=t['DTm'][:, :], op=ALU.mult), reads=[p3, t['DTm']], writes=[t['AT']])
                    p5, p6, p7, p8 = pp.next(), pp.next(), pp.next(), pp.next()
                    kb.mm(p5, p5[:, :], t['wT'], t['wT'][:, :], St[hd], St[hd][:, :])
                    kb.mm(p6, p6[:, :], q_, qT, St[hd], St[hd][:, :])
                    kb.op('dve', lambda e: e.tensor_tensor(out=t['vn'][:, :], in0=t['u'][:, :], in1=p5[:, :], op=ALU.subtract), reads=[t['u'], p5], writes=[t['vn']])
                    kb.mm(p7, p7[:, :], t['AT'], t['AT'][:, :], t['vn'], t['vn'][:, :])
                    kb.mm(p8, p8[:, :], t['Kp'], t['Kp'][:, :], t['vn'], t['vn'][:, :])
                    kb.act(t['o2'], t['o2'][:, :], p7, p7[:, :], AF.Copy)
                    kb.op('dve', lambda e: e.scalar_tensor_tensor(out=o_[:, hd, :], in0=p6[:, :], scalar=s_[:, 1, hd:hd + 1], in1=t['o2'][:, :],
                                                                  op0=ALU.mult, op1=ALU.add), reads=[p6, s_, t['o2']], writes=[o_])
                    kb.op('dve', lambda e: e.scalar_tensor_tensor(out=St[hd][:, :], in0=St[hd][:, :], scalar=s_[:, 3, hd:hd + 1], in1=p8[:, :],
                                                                  op0=ALU.mult, op1=ALU.add), reads=[St[hd], s_, p8], writes=[St[hd]])
                if not rev:
                    kb.dma('pool', S['OF'][a0:a0 + 128, :], o_[:, :, :].rearrange("p h f -> p (h f)"), reads=[o_])
                else:
                    f_ = oft.next()
                    kb.dma('sp', f_[:, :, :], S['OF'][a0:a0 + 128, :].rearrange("p (h f) -> p h f", f=128), writes=[f_])
                    kb.op('dve', lambda e: e.tensor_tensor(out=o_[:, :, :], in0=o_[:, :, :], in1=f_[:, :, :], op=ALU.add), reads=[o_, f_], writes=[o_])
                    kb.dma('pool', S['OS'][a0:a0 + 128, :], o_[:, :, :].rearrange("p h f -> p (h f)"), reads=[o_])
            t0 += Sq
        kb.barrier()
    kb.ph = None


DGROUPS = ((128, 1), (512, 4), (2048, 16))
EV_PAD = 640
EV_L = 2 * 1024 + 1 + 2 * EV_PAD


def t5_bucket_np(rel):
    nb, max_exact = 16, 8
    n = np.abs(rel)
    large = max_exact + (np.log(np.maximum(n, 1) / max_exact) / math.log(1024 / max_exact) * (nb - max_exact)).astype(np.int64)
    large = np.minimum(large, nb - 1)
    return (np.where(rel > 0, nb, 0) + np.where(n < max_exact, n, large)).astype(np.int32)


def phase_o3(kb, P, C, S, seqs):
    with ExitStack() as ph:
        kb.ph = ph
        rb = kb.sb([32, 12], F32, "rb")
        kb.dma('sp', rb[:, :], P['rel_bias'], writes=[rb])
        oh = kb.sb([32, 3, 129], F32, "oh")
        kb.dma('sp', oh[:, :, :], C['onehot'], writes=[oh])
        ev = kb.sb([4, 3, EV_L], F32, "ev")
        kb.op('dve', lambda e: e.memset(ev[:, :, :], 0.0), writes=[ev])
        pbv = kb.ps([4, 129], F32, "pbv")
        for gi, (win, dil) in enumerate(DGROUPS):
            kb.mm(pbv, pbv[:, :], rb, rb[:, gi * 4:(gi + 1) * 4], oh, oh[:, gi, :])
            c0 = EV_PAD + 1024 - 64 * dil
            kb.act(ev, ev[:, gi, c0:c0 + 128 * dil + 1:dil], pbv, pbv[:, :], AF.Exp)
        kb.dma('pool', S['EV'], ev[:, :, :], reads=[ev])
        kb.barrier()
        rng_ = []
        for gi, (win, dil) in enumerate(DGROUPS):
            w = 64 * dil
            lo = -((w + 127) // 128)
            hi = 4 + (w + 127) // 128
            rng_.append((lo, hi))
        ntile = sum(hi - lo for lo, hi in rng_)
        Et = kb.sb([128, ntile, 512], F32, "Et")
        hk = Rot([kb.sb([128, 512], F32, "hk") for _ in range(2)])
        kt = Rot([kb.sb([128, 3, 2560], BF16, "kt") for _ in range(2)])
        vt = Rot([kb.sb([128, 3, 20, 65], BF16, "vt") for _ in range(2)])
        qt = Rot([kb.sb([128, 3, 512], BF16, "qt") for _ in range(2)])
        es = Rot([kb.sb([128, 512], F32, "es") for _ in range(3)])
        pe_ = Rot([kb.sb([128, 512], BF16, "pexp") for _ in range(3)])
        ones = kb.sb([128, 64], F32, "ones")
        kb.op('dve', lambda e: e.memset(ones[:, :], 1.0), writes=[ones])
        num = Rot([kb.sb([65, 512], F32, "num") for _ in range(2)])
        rec = Rot([kb.sb([64, 512], F32, "rec") for _ in range(2)])
        yt = Rot([kb.sb([64, 512], BF16, "yt") for _ in range(2)])
        pss = Rot([kb.ps([128, 512], F32, "pss") for _ in range(4)])
        pac = Rot([kb.ps([65, 512], F32, "pac") for _ in range(2)])
        pdn = Rot([kb.ps([64, 512], F32, "pdn") for _ in range(1)])
        for slot in range(4):
            ti = 0
            for gi, (win, dil) in enumerate(DGROUPS):
                lo, hi = rng_[gi]
                for r in range(lo, hi):
                    h_ = hk.next()
                    base = r * 128 + EV_PAD + 1024 - 511
                    src = bass.AP(S['EV'].tensor, (slot * 3 + gi) * EV_L + base, [[1, 128], [1, 512]])
                    kb.dma('sp', h_[:, :], src, writes=[h_])
                    kb.op('dve', lambda e: e.tensor_copy(out=Et[:, ti, :], in_=h_[:, ::-1]), reads=[h_], writes=[Et])
                    ti += 1
            t0 = 0
            for Sq in seqs:
                for qi in range(Sq // 512):
                    q0 = qi * 512
                    q_, k_, v_ = qt.next(), kt.next(), vt.next()
                    spans = []
                    for gi in range(3):
                        hh = gi * 4 + slot
                        ch, pb = hh // 2, (hh % 2) * 64
                        lo, hi = rng_[gi]
                        klo, khi = max(q0 + lo * 128, 0), min(q0 + hi * 128, Sq)
                        spans.append((klo, khi, pb))
                        nk = khi - klo
                        kb.dma('sp', q_[pb:pb + 64, gi, :], S['DQK'][hh * 64:(hh + 1) * 64, t0 + q0:t0 + q0 + 512], writes=[q_])
                        kb.dma('sp', k_[pb:pb + 64, gi, 0:nk], S['DQK'][768 + hh * 64:768 + (hh + 1) * 64, t0 + klo:t0 + khi], writes=[k_])
                        for v0 in range(0, nk // 128, 8):
                            v1 = min(v0 + 8, nk // 128)
                            kb.dma('sp', v_[:, gi, v0:v1, :],
                                   S['DV'][t0 + klo + v0 * 128:t0 + klo + v1 * 128, hh * 65:(hh + 1) * 65].rearrange("(k p) f -> p k f", p=128),
                                   writes=[v_])
                    acc = pac.next()
                    total = sum((khi - klo) // 128 for klo, khi, _ in spans)
                    cnt = 0
                    tb = 0
                    for gi in range(3):
                        lo, hi = rng_[gi]
                        klo, khi, pb = spans[gi]
                        for kk in range((khi - klo) // 128):
                            r = (klo + kk * 128 - q0) // 128
                            e_idx = tb + (r - lo)
                            p_s, e_s, p_e = pss.next(), es.next(), pe_.next()
                            kb.mm(p_s, p_s[:, :], k_, k_[pb:pb + 64, gi, kk * 128:(kk + 1) * 128], q_, q_[pb:pb + 64, gi, :])
                            kb.act(e_s, e_s[:, :], p_s, p_s[:, :], AF.Exp)
                            kb.op('dve', lambda e: e.tensor_tensor(out=p_e[:, :], in0=e_s[:, :], in1=Et[:, e_idx, :], op=ALU.mult),
                                  reads=[e_s, Et], writes=[p_e])
                            kb.mm(acc, acc[:, :], v_, v_[:, gi, kk, :], p_e, p_e[:, :], start=(cnt == 0), stop=(cnt == total - 1))
                            cnt += 1
                        tb += hi - lo
                    n_, r_, y_, p_d = num.next(), rec.next(), yt.next(), pdn.next()
                    kb.act(n_, n_[:, :], acc, acc[:, :], AF.Copy)
                    kb.mm(p_d, p_d[:, :], ones, ones[64:65, 0:64], n_, n_[64:65, :])
                    kb.op('dve', lambda e: e.reciprocal(out=r_[:, :], in_=p_d[:, :]), reads=[p_d], writes=[r_])
                    kb.op('dve', lambda e: e.tensor_tensor(out=y_[:, :], in0=n_[0:64, :], in1=r_[:, :], op=ALU.mult), reads=[n_, r_], writes=[y_])
                    kb.dma('pool', S['YD'][slot * 64:(slot + 1) * 64, t0 + q0:t0 + q0 + 512], y_[:, :], reads=[y_])
                t0 += Sq
        kb.barrier()
    kb.ph = None


def phase_o4(kb, P, C, S, j, T):
    with ExitStack() as ph:
        kb.ph = ph
        ident_bf, ident_f, epsb = make_consts(kb, C)
        gn = load_bcast(kb, P['c_norm'][j], 128, "gn")
        ost = Rot([kb.sb([128, 4, 128], F32, "os") for _ in range(2)])
        zst = Rot([kb.sb([128, 4, 128], F32, "zs") for _ in range(2)])
        sq = Rot([kb.sb([128, 4, 128], F32, "sq") for _ in range(2)])
        ss = Rot([kb.sb([128, 4], F32, "ss") for _ in range(2)])
        rs = Rot([kb.sb([128, 4], F32, "rs") for _ in range(2)])
        yb = Rot([kb.sb([128, 4, 128], BF16, "yb") for _ in range(2)])
        yT = Rot([kb.sb([128, 4, 128], BF16, "yT") for _ in range(2)])
        pt = Rot([kb.ps([128, 4, 128], BF16, "pt") for _ in range(2)])
        for it in range(T // 128):
            r0 = it * 128
            o_, z_, s_, s1, r1, y_, yT_, p_ = ost.next(), zst.next(), sq.next(), ss.next(), rs.next(), yb.next(), yT.next(), pt.next()
            kb.dma('sp', o_[:, :, :], S['OS'][r0:r0 + 128, :].rearrange("p (h f) -> p h f", f=128), writes=[o_])
            kb.dma('sp', z_[:, :, :], S['ZS'][r0:r0 + 128, :].rearrange("p (h f) -> p h f", f=128), writes=[z_])
            kb.act(s_, s_[:, :, :], o_, o_[:, :, :], AF.Square)
            kb.op('dve', lambda e: e.tensor_reduce(out=s1[:, :], in_=s_[:, :, :], axis=AX.X, op=ALU.add), reads=[s_], writes=[s1])
            kb.act(s1, s1[:, :], s1, s1[:, :], AF.Sqrt, scale=1.0 / 128, bias=epsb[:, :], extra_reads=[epsb])
            kb.op('dve', lambda e: e.reciprocal(out=r1[:, :], in_=s1[:, :]), reads=[s1], writes=[r1])
            kb.op('dve', lambda e: e.tensor_tensor(out=o_[:, :, :], in0=o_[:, :, :], in1=r1[:, :].unsqueeze(2).to_broadcast([128, 4, 128]), op=ALU.mult),
                  reads=[o_, r1], writes=[o_])
            kb.op('dve', lambda e: e.tensor_tensor(out=o_[:, :, :], in0=o_[:, :, :], in1=gn[:, :].unsqueeze(1).to_broadcast([128, 4, 128]), op=ALU.mult),
                  reads=[o_, gn], writes=[o_])
            kb.op('dve', lambda e: e.tensor_tensor(out=y_[:, :, :], in0=o_[:, :, :], in1=z_[:, :, :], op=ALU.mult), reads=[o_, z_], writes=[y_])
            for c in range(4):
                kb.op('pe', lambda e: e.transpose(p_[:, c, :], y_[:, c, :], ident_bf[:, :]), reads=[y_, ident_bf], writes=[p_])
            kb.act(yT_, yT_[:, :, :], p_, p_[:, :, :], AF.Copy)
            kb.dma('pool', S['YC'][:, r0:r0 + 128].rearrange("(c p) t -> p c t", p=128), yT_[:, :, :], reads=[yT_])
        kb.barrier()
    kb.ph = None


def build(seqs, plan):
    T = sum(seqs)
    kb = KB()
    nc = kb.nc
    P = {}
    shapes = dict(rel_bias=(32, 12), norm_mix=(4, D), norm_ff=(4, D), norm_final=(D,), w_ff1=(4, D, 4 * D),
                  w_ff2=(4, 4 * D, D), w_in_e=(2, D, 1792), w_out_e=(2, D, D), a_qnorm=(2, 64), a_knorm=(2, 64),
                  b_conv_w=(2, 4, 512), b_conv_b=(2, 512), b_wr=(2, 2, 8, 64, 64), b_br=(2, 2, 512),
                  b_wi=(2, 2, 8, 64, 64), b_bi=(2, 2, 512), b_lambda=(2, 2, 512), w_in_o=(2, D, 4368),
                  w_out_o=(2, 768, D), c_conv_w=(2, 4, 1536), c_a_log=(2, 2, 4), c_dt_bias=(2, 2, 4), c_norm=(2, 128))
    for k, sh in shapes.items():
        P[k] = kb.dram(k, sh, F32, kind="ExternalInput")
    x_in = kb.dram("x", (T, D), F32, kind="ExternalInput")
    C = {'ident': kb.dram("ident", (128, 128), F32, kind="ExternalInput"),
         'rope': kb.dram("rope", (T, 64), F32, kind="ExternalInput")}
    S = {'QT': kb.dram("s_qt", (512, T), BF16), 'KTD': kb.dram("s_ktd", (256, T), BF16), 'VA': kb.dram("s_va", (T, 130), BF16),
         'XR': kb.dram("s_xr", (512, T), F32), 'GG': kb.dram("s_gg", (512, T), F32), 'HF': kb.dram("s_hf", (512, T), F32),
         'YB': kb.dram("s_yb", (512, T), BF16), 'YA': kb.dram("s_ya", (512, T), BF16),
         'QKV': kb.dram("s_qkv", (1536, T), F32), 'ZS': kb.dram("s_zs", (T, 512), F32), 'BG': kb.dram("s_bg", (T, 16), F32),
         'DQK': kb.dram("s_dqk", (1536, T), BF16), 'DV': kb.dram("s_dv", (T, 780), BF16),
         'CQK': kb.dram("s_cqk", (1024, T), F32), 'CKV': kb.dram("s_ckv", (T, 1024), F32),
         'OF': kb.dram("s_of", (T, 512), F32), 'OS': kb.dram("s_os", (T, 512), F32),
         'EV': kb.dram("s_ev", (4, 3, EV_L), F32), 'YD': kb.dram("s_yd", (256, T), BF16), 'YC': kb.dram("s_yc", (512, T), BF16)}
    C['ba'] = kb.dram("c_ba", (2, 128, 128), F32, kind="ExternalInput")
    C['ms'] = kb.dram("c_ms", (2, 128, 128), F32, kind="ExternalInput")
    C['onehot'] = kb.dram("c_onehot", (32, 3, 129), F32, kind="ExternalInput")
    C['km'] = kb.dram("c_km", (5, 128, 128), F32, kind="ExternalInput")
    y = kb.dram("y", (T, D), F32, kind="ExternalOutput")
    xs = kb.dram("xs", (T, D), F32)
    cur = x_in
    for i, (name, layer) in enumerate(plan):
        last = (i == len(plan) - 1)
        if name == 'M':
            phase_mlp(kb, P, C, cur, xs, layer, T, final_out=(y if last else None))
            cur = xs
        elif name == 'E':
            j = layer // 2
            phase_e1(kb, P, C, S, cur, j, layer, T)
            phase_e2(kb, P, C, S, j, seqs, False)
            phase_e2(kb, P, C, S, j, seqs, True)
            phase_e3(kb, P, C, S, seqs)
            phase_outproj(kb, P, C, S, P['w_out_e'][j], [(S['YA'], 4), (S['YB'], 4)], cur, (y if last else xs), T)
            cur = xs
        elif name == 'O':
            j = layer // 2
            phase_o1(kb, P, C, S, cur, j, layer, T)
            if O_STOP >= 2:
                phase_o2a(kb, P, C, S, j, seqs)
            if O_STOP >= 3:
                phase_o2b(kb, P, C, S, seqs, False)
            if O_STOP >= 4:
                phase_o2b(kb, P, C, S, seqs, True)
            if O_STOP >= 5:
                phase_o3(kb, P, C, S, seqs)
            if O_STOP >= 6:
                phase_o4(kb, P, C, S, j, T)
            if O_STOP >= 7:
                phase_outproj(kb, P, C, S, P['w_out_o'][j], [(S['YC'], 4), (S['YD'], 2)], cur, (y if last else xs), T)
            cur = xs
    kb.barrier()
    kb.stack.close()
    return kb


SEQS = [16384, 2048, 2048]
PLAN = [('E', 0), ('M', 0), ('O', 1), ('M', 1), ('E', 2), ('M', 2), ('O', 3), ('M', 3)]


def host_consts(seqs=None):
    seqs = SEQS if seqs is None else seqs
    tabs = []
    for Sq in seqs:
        pos = np.arange(Sq)
        row = (pos // 64).astype(np.float32)
        col = (pos % 64).astype(np.float32)
        inv = (np.float32(10000.0) ** (-np.arange(16, dtype=np.float32) / np.float32(16))).astype(np.float32)
        ang = np.concatenate([row[:, None] * inv[None], col[:, None] * inv[None]], axis=1).astype(np.float32)
        tabs.append(np.concatenate([np.cos(ang), np.sin(ang)], axis=1).astype(np.float32))
    k_ = np.arange(128)[:, None]
    i_ = np.arange(128)[None, :]
    ba = np.stack([(k_ <= i_), (k_ >= i_)]).astype(np.float32)
    ms = np.stack([(i_ < k_), (i_ > k_)]).astype(np.float32)
    onehot = np.zeros((32, 3, 129), np.float32)
    for gi, (win, dil) in enumerate(DGROUPS):
        b = t5_bucket_np(np.arange(-64, 65) * dil)
        onehot[b, gi, np.arange(129)] = 1.0
    km = [(k_ // 8 == i_ // 8)]
    for sz in (8, 16, 32, 64):
        km.append((k_ // (2 * sz) == i_ // (2 * sz)) & (k_ // sz != i_ // sz))
    km = np.stack(km).astype(np.float32)
    return {'ident': np.eye(128, dtype=np.float32), 'rope': np.ascontiguousarray(np.concatenate(tabs, 0)),
            'c_ba': ba, 'c_ms': ms, 'c_onehot': onehot, 'c_km': km}


def kernel(**inputs):
    xp = np.asarray(inputs['x_prompt'])
    xsm = np.asarray(inputs['x_sample'])
    kb = build(SEQS, PLAN)
    params = {k: np.ascontiguousarray(np.asarray(v), dtype=np.float32) for k, v in inputs.items()
              if k not in ('x_prompt', 'x_sample')}
    hc = host_consts()
    in_maps = []
    for c in range(8):
        x = np.concatenate([xp[c % 2], xsm[2 * c], xsm[2 * c + 1]], axis=0)
        m = dict(params)
        m.update(hc)
        m['x'] = np.ascontiguousarray(x)
        in_maps.append(m)
    res = run_bass_kernel_spmd(kb.nc, in_maps, core_ids=list(range(8)))
    yp = np.stack([res.results[b]['y'][:16384] for b in range(2)], axis=0)
    ys = np.stack([res.results[c // 2]['y'][16384 + (c % 2) * 2048:16384 + (c % 2 + 1) * 2048] for c in range(16)], axis=0)
    return (yp.astype(np.float32), ys.astype(np.float32))
```
